# Optimizing a Trainium2 kernel written in Bass

```python
import jax, jax.numpy as jnp
from jax import lax
import numpy as np

D_MODEL = 2048
BATCH = 4
SEQ = 8192
DEPTH = 2
DEC_BATCH = 8
DEC_SEQ = 4096
PAST_LEN = 128

GRID_W = 64
POOL_WINDOWS = (2, 4, 8, 16)
POOL_GROUPS = len(POOL_WINDOWS)
D_POOL = D_MODEL // 2
POOL_GC = D_POOL // POOL_GROUPS
CHUNK = 128
SGU_GROUPS = 8
D_SGU = D_MODEL // 2
SGU_GC = D_SGU // SGU_GROUPS
NA_HEADS = 16
NA_HEAD_DIM = 64
D_NA = NA_HEADS * NA_HEAD_DIM
NA_KH_MAX = 8
NA_KW = 16
N_BRANCH = 3
D_FF = 4 * D_MODEL
D_IN = D_POOL + 2 * D_SGU + 3 * D_NA + N_BRANCH * D_MODEL
SPLITS = tuple(int(i) for i in np.cumsum([D_POOL, D_SGU, D_SGU, D_NA, D_NA, D_NA]))
DN_ALPHA = (2 * DEPTH) ** 0.25
DN_BETA = (8 * DEPTH) ** -0.25
LN_EPS = 1e-5

kernel_name = "hybrid_pool_sgu_natten_encoder"


def layer_norm(x, g, b):
    x32 = x.astype(jnp.float32)
    mu = jnp.mean(x32, axis=-1, keepdims=True)
    var = jnp.mean(jnp.square(x32 - mu), axis=-1, keepdims=True)
    y = (x32 - mu) * lax.rsqrt(var + LN_EPS)
    return (y * g.astype(jnp.float32) + b.astype(jnp.float32)).astype(x.dtype)


def pool_mixer(a, w_pool, s_pool):
    bsz, s, _ = a.shape
    a32 = a.astype(jnp.float32).reshape(bsz, s, POOL_GROUPS, POOL_GC)
    cs = jnp.concatenate([jnp.zeros((bsz, 1, POOL_GROUPS, POOL_GC), jnp.float32),
                          jnp.cumsum(a32, axis=1)], axis=1)
    t = np.arange(s)
    outs = []
    for gi, w in enumerate(POOL_WINDOWS):
        lo = np.clip(t - w // 2, 0, s)
        hi = np.clip(t + w // 2, 0, s)
        inv_cnt = (1.0 / (hi - lo)).astype(np.float32)[None, :, None]
        csg = cs[:, :, gi]
        outs.append((csg[:, hi] - csg[:, lo]) * inv_cnt - a32[:, :, gi])
    p = jnp.stack(outs, axis=2).astype(a.dtype)
    y = jnp.einsum('bsgc,gcd->bsgd', p, w_pool).reshape(bsz, s, D_POOL)
    return y * s_pool


def sgu_mixer(u, v, ln_g, ln_b, w_s, b_s):
    bsz, s, _ = u.shape
    u = jax.nn.gelu(u)
    v = layer_norm(jax.nn.gelu(v), ln_g, ln_b)
    vc = v.reshape(bsz, s // CHUNK, CHUNK, SGU_GROUPS, SGU_GC)
    sg = jnp.einsum('gpq,bnqgc->bnpgc', w_s, vc) + b_s.T[None, None, :, :, None]
    return u * sg.reshape(bsz, s, D_SGU)


def neighbourhood_attention(q, k, v, rpb):
    bsz, s, h, dh = q.shape
    rows = s // GRID_W
    kh = min(NA_KH_MAX, rows)
    qg = q.reshape(bsz, rows, GRID_W, h, dh)
    kg = k.reshape(bsz, rows, GRID_W, h, dh)
    vg = v.reshape(bsz, rows, GRID_W, h, dh)
    col = np.arange(GRID_W)
    c0 = np.clip(col - NA_KW // 2, 0, GRID_W - NA_KW)
    col_idx = c0[:, None] + np.arange(NA_KW)[None, :]
    dc = col_idx - col[:, None]
    scale = dh ** -0.5

    def one_row(r):
        r0 = jnp.clip(r - kh // 2, 0, rows - kh)
        q_r = lax.dynamic_index_in_dim(qg, r, axis=1, keepdims=False)
        k_band = lax.dynamic_slice_in_dim(kg, r0, kh, axis=1)
        v_band = lax.dynamic_slice_in_dim(vg, r0, kh, axis=1)
        k_sel = k_band[:, :, col_idx]
        v_sel = v_band[:, :, col_idx]
        dr = r0 + jnp.arange(kh) - r
        bias = rpb[:, dr[:, None, None] + (NA_KH_MAX - 1), dc[None] + (NA_KW - 1)]
        sc = jnp.einsum('bqhd,biqjhd->bhqij', q_r, k_sel).astype(jnp.float32) * scale
        sc = sc + bias.astype(jnp.float32).transpose(0, 2, 1, 3)[None]
        p = jax.nn.softmax(sc.reshape(bsz, h, GRID_W, kh * NA_KW), axis=-1)
        p = p.reshape(bsz, h, GRID_W, kh, NA_KW).astype(v.dtype)
        return jnp.einsum('bhqij,biqjhd->bqhd', p, v_sel)

    out = lax.map(one_row, jnp.arange(rows))
    return out.transpose(1, 0, 2, 3, 4).reshape(bsz, s, h * dh)


def mixing_sublayer(x, w_in, w_pool, s_pool, sgu_ln_g, sgu_ln_b, w_s, b_s, rpb,
                    w_br_pool, w_br_sgu, w_br_na, w_out):
    bsz, s, _ = x.shape
    z = x @ w_in
    a, u, v, q, k, vv, g = jnp.split(z, SPLITS, axis=-1)
    y_a = pool_mixer(a, w_pool, s_pool) @ w_br_pool
    y_b = sgu_mixer(u, v, sgu_ln_g, sgu_ln_b, w_s, b_s) @ w_br_sgu
    hd = (bsz, s, NA_HEADS, NA_HEAD_DIM)
    y_c = neighbourhood_attention(q.reshape(hd), k.reshape(hd), vv.reshape(hd), rpb) @ w_br_na
    gates = jax.nn.sigmoid(g.reshape(bsz, s, N_BRANCH, D_MODEL))
    merged = gates[:, :, 0] * y_a + gates[:, :, 1] * y_b + gates[:, :, 2] * y_c
    return merged @ w_out


def trunk(x, w_in, w_pool, s_pool, sgu_ln_g, sgu_ln_b, w_s, b_s, rpb,
          w_br_pool, w_br_sgu, w_br_na, w_out, ln1_g, ln1_b, w_up, w_down, ln2_g, ln2_b):
    for l in range(DEPTH):
        m = mixing_sublayer(x, w_in[l], w_pool[l], s_pool[l], sgu_ln_g[l], sgu_ln_b[l], w_s[l], b_s[l],
                            rpb[l], w_br_pool[l], w_br_sgu[l], w_br_na[l], w_out[l])
        x = layer_norm(DN_ALPHA * x + m, ln1_g[l], ln1_b[l])
        f = jnp.square(jax.nn.relu(x @ w_up[l])) @ w_down[l]
        x = layer_norm(DN_ALPHA * x + f, ln2_g[l], ln2_b[l])
    return x


def setup_inputs(seed: int = 0) -> dict:
    key = jax.random.key(seed)
    ks = jax.random.split(key, 24)
    f32 = jnp.float32
    nrm = lambda k, shape, sc: jax.random.normal(k, shape, f32) * sc
    L = DEPTH
    return {
        "x_prompt": nrm(ks[0], (BATCH, SEQ, D_MODEL), 1.0),
        "x_sample": nrm(ks[1], (DEC_BATCH, DEC_SEQ, D_MODEL), 1.0),
        "w_in": nrm(ks[2], (L, D_MODEL, D_IN), D_MODEL ** -0.5),
        "w_pool": nrm(ks[3], (L, POOL_GROUPS, POOL_GC, POOL_GC), POOL_GC ** -0.5),
        "s_pool": 1.0 + nrm(ks[4], (L, D_POOL), 0.1),
        "sgu_ln_g": 1.0 + nrm(ks[5], (L, D_SGU), 0.01),
        "sgu_ln_b": nrm(ks[6], (L, D_SGU), 0.01),
        "w_s": nrm(ks[7], (L, SGU_GROUPS, CHUNK, CHUNK), CHUNK ** -0.5),
        "b_s": 1.0 + nrm(ks[8], (L, SGU_GROUPS, CHUNK), 0.01),
        "rpb": nrm(ks[9], (L, NA_HEADS, 2 * NA_KH_MAX - 1, 2 * NA_KW - 1), 0.1),
        "w_br_pool": nrm(ks[10], (L, D_POOL, D_MODEL), DN_BETA * D_POOL ** -0.5),
        "w_br_sgu": nrm(ks[11], (L, D_SGU, D_MODEL), DN_BETA * D_SGU ** -0.5),
        "w_br_na": nrm(ks[12], (L, D_NA, D_MODEL), DN_BETA * D_NA ** -0.5),
        "w_out": nrm(ks[13], (L, D_MODEL, D_MODEL), DN_BETA * D_MODEL ** -0.5),
        "ln1_g": 1.0 + nrm(ks[14], (L, D_MODEL), 0.01),
        "ln1_b": nrm(ks[15], (L, D_MODEL), 0.01),
        "w_up": nrm(ks[16], (L, D_MODEL, D_FF), DN_BETA * D_MODEL ** -0.5),
        "w_down": nrm(ks[17], (L, D_FF, D_MODEL), DN_BETA * D_FF ** -0.5),
        "ln2_g": 1.0 + nrm(ks[18], (L, D_MODEL), 0.01),
        "ln2_b": nrm(ks[19], (L, D_MODEL), 0.01),
    }


def reference(x_prompt, x_sample, w_in, w_pool, s_pool, sgu_ln_g, sgu_ln_b, w_s, b_s, rpb,
              w_br_pool, w_br_sgu, w_br_na, w_out, ln1_g, ln1_b, w_up, w_down, ln2_g, ln2_b):
    y_prompt = trunk(x_prompt, w_in, w_pool, s_pool, sgu_ln_g, sgu_ln_b, w_s, b_s, rpb,
                     w_br_pool, w_br_sgu, w_br_na, w_out, ln1_g, ln1_b, w_up, w_down, ln2_g, ln2_b)
    y_sample = trunk(x_sample, w_in, w_pool, s_pool, sgu_ln_g, sgu_ln_b, w_s, b_s, rpb,
                     w_br_pool, w_br_sgu, w_br_na, w_out, ln1_g, ln1_b, w_up, w_down, ln2_g, ln2_b)
    return (y_prompt, y_sample)
```

```python
import os
import numpy as np
from contextlib import ExitStack
import concourse.bass as bass
import concourse.mybir as mybir
from concourse.bass_utils import run_bass_kernel_spmd

F32 = mybir.dt.float32
BF16 = mybir.dt.bfloat16
AF = mybir.ActivationFunctionType
ALU = mybir.AluOpType

ENGS = ['pe', 'act', 'dve', 'pool', 'sp']
NDSEM = 32
DMA_MAXD = int(os.environ.get('K_MAXD', 512))

D = 2048
DIN = 12288
DFF = 8192
NT = 8192
L = 2
ALPHA = float((2 * L) ** 0.25)
EPS = 1e-5
NEG = -30000.0


class Op:
    __slots__ = ('fn', 'waits', 'signal', 'dma', 'epoch', 'sigval', 'didx')

    def __init__(self, fn, waits, dma, epoch, didx):
        self.fn = fn; self.waits = waits; self.signal = False; self.dma = dma
        self.epoch = epoch; self.sigval = 0; self.didx = didx


class Prog:
    def __init__(self):
        self.ops = {e: [] for e in ENGS}
        self.lastw = {}
        self.readers = {}
        self.waited = {e: {} for e in ENGS}
        self.ndma_g = 0
        self.epoch = 0
        self.last_tok = {}

    def _add(self, eng, fn, reads, writes, dma):
        deps = {}

        def need(tok):
            k = (tok[0], tok[1])
            if deps.get(k, -1) < tok[2]:
                deps[k] = tok[2]
        for k in reads:
            t = self.lastw.get(k)
            if t is not None:
                need(t)
        for k in writes:
            t = self.lastw.get(k)
            if t is not None:
                need(t)
            r = self.readers.get(k)
            if r:
                for kk, idx in r.items():
                    need((kk[0], kk[1], idx))
        waits = []
        wd = self.waited[eng]
        for k2, idx in deps.items():
            if k2[0] == 'E' and k2[1] == eng and eng == 'pe':
                continue
            if wd.get(k2, -1) >= idx:
                continue
            wd[k2] = idx
            waits.append((k2[0], k2[1], idx))
        if dma:
            didx = self.ndma_g
            self.ndma_g += 1
            slot = didx % NDSEM
            tok = ('D', slot, didx)
            if didx >= NDSEM and wd.get(('D', slot), -1) < didx - NDSEM:
                wd[('D', slot)] = didx - NDSEM
                waits.append(('D', slot, didx - NDSEM))
        else:
            didx = -1
            tok = ('E', eng, len(self.ops[eng]))
        self.ops[eng].append(Op(fn, waits, dma, self.epoch, didx))
        self.last_tok[(tok[0], tok[1])] = tok
        k2 = (tok[0], tok[1])
        for k in reads:
            r = self.readers.setdefault(k, {})
            r[k2] = tok[2]
        for k in writes:
            self.lastw[k] = tok
            self.readers[k] = {}
        return tok

    def op(self, eng, fn, reads=(), writes=()):
        return self._add(eng, fn, reads, writes, False)

    def dma(self, out, in_, reads=(), writes=(), eng='sp'):
        return self._add(eng, lambda e: e.dma_start(out=out, in_=in_), reads, writes, True)

    def barrier(self):
        toks = list(self.last_tok.values())
        for x in ENGS:
            waits = []
            for t in toks:
                if t[0] == 'E' and t[1] == x and x == 'pe':
                    continue
                k2 = (t[0], t[1])
                if self.waited[x].get(k2, -1) >= t[2]:
                    continue
                waits.append(t)
            if waits:
                self.ops[x].append(Op(None, waits, False, self.epoch, -1))
        self.epoch += 1
        self.lastw = {}
        self.readers = {}
        self.waited = {e: {} for e in ENGS}
        self.last_tok = {}

    def emit(self, nc):
        for e in ENGS:
            for op in self.ops[e]:
                for (kind, e2, idx) in op.waits:
                    if kind == 'E':
                        self.ops[e2][idx].signal = True
        for e in ENGS:
            cnt = {}
            for op in self.ops[e]:
                if op.signal:
                    cnt[op.epoch] = cnt.get(op.epoch, 0) + 1
                op.sigval = cnt.get(op.epoch, 0)
        stats = {e: (len(self.ops[e]), sum(len(o.waits) for o in self.ops[e])) for e in ENGS}
        print("PROG ops/waits:", stats, "epochs", self.epoch + 1, flush=True)
        with ExitStack() as st:
            csem = {}
            dsem = {}
            used = set()
            for e in ENGS:
                for op in self.ops[e]:
                    if op.signal:
                        used.add(('E', e, op.epoch))
            for (kind, e, ep) in sorted(used):
                csem[(e, ep)] = st.enter_context(nc.semaphore(f"s{kind}_{e}_{ep}"))
            for i in range(NDSEM):
                dsem[i] = st.enter_context(nc.semaphore(f"sD_{i}"))
            block = st.enter_context(nc.Block())
            ops = self.ops

            def run(ename, eobj):
                for op in ops[ename]:
                    for (kind, e2, idx) in op.waits:
                        if kind == 'E':
                            o2 = ops[e2][idx]
                            eobj.wait_ge(csem[(e2, o2.epoch)], o2.sigval)
                        else:
                            eobj.wait_ge(dsem[e2], 16 * (idx // NDSEM + 1))
                    if op.fn is None:
                        continue
                    ins = op.fn(eobj)
                    if op.dma:
                        ins.then_inc(dsem[op.didx % NDSEM], 16)
                    elif op.signal:
                        ins.then_inc(csem[(ename, op.epoch)], 1)

            @block.tensor
            def _(t):
                run('pe', t)

            @block.scalar
            def _(a):
                run('act', a)

            @block.vector
            def _(v):
                run('dve', v)

            @block.gpsimd
            def _(g):
                run('pool', g)

            @block.sync
            def _(s):
                run('sp', s)


ARENA_F32 = 46 * 1024


class Builder:
    def __init__(self, stop_after=None, dump=()):
        self.stop_after = stop_after
        self.dump = set(dump)
        self.nc = bass.Bass("TRN2", target_bir_lowering=False)
        self.P = Prog()
        self.hb = 0
        self.rot = 0

    def din(self, name, shape, dt=F32):
        return self.nc.dram_tensor(name, list(shape), dt, kind="ExternalInput").ap()

    def dscr(self, name, shape, dt):
        if name in self.dump:
            return self.nc.dram_tensor(name, list(shape), dt, kind="ExternalOutput").ap()
        return self.nc.dram_tensor(name, list(shape), dt).ap()

    def areset(self):
        self.aoff = 0

    def f32(self, n):
        v = self.arena[:, self.aoff:self.aoff + n]
        self.aoff += n
        assert self.aoff <= ARENA_F32, self.aoff
        return v

    def bf(self, n):
        assert n % 2 == 0
        v = self.arena[:, self.aoff:self.aoff + n // 2].bitcast(BF16)
        self.aoff += n // 2
        assert self.aoff <= ARENA_F32, self.aoff
        return v

    def nb(self):
        i = self.hb % 16
        self.hb += 1
        return self.ps[:, i * 256:(i + 1) * 256], [('bank', i // 2)]

    def nbfull(self):
        if self.hb % 2:
            self.hb += 1
        i = self.hb % 16
        self.hb += 2
        return self.ps[:, i * 256:(i + 2) * 256], [('bank', i // 2)]

    def tiles(self, n, sub=1):
        v = os.environ.get("K_TILES")
        if not v:
            return list(range(n * sub))
        base = [int(t) for t in v.split(",")]
        return [t * sub + k for t in base for k in range(sub)]

    def eng2(self):
        self.rot += 1
        return ('act', 'dve')[self.rot % 2]

    def eng3(self):
        self.rot += 1
        return ('act', 'dve', 'pool')[self.rot % 3]

    def MM(self, out, lhsT, rhs, st, sp, rd, wr):
        self.P.op('pe', lambda e: e.matmul(out, lhsT, rhs, start=st, stop=sp), rd, wr)

    def TR(self, out, in_, ident, rd, wr):
        self.P.op('pe', lambda e: e.transpose(out, in_, ident), rd, wr)

    def ACT(self, out, in_, func, rd, wr, scale=1.0, bias=None):
        if bias is None:
            self.P.op('act', lambda e: e.activation(out=out, in_=in_, func=func, scale=scale), rd, wr)
        else:
            self.P.op('act', lambda e: e.activation(out=out, in_=in_, func=func, scale=scale, bias=bias), rd, wr)

    def CP(self, eng, out, in_, rd, wr):
        if eng == 'act':
            self.P.op('act', lambda e: e.copy(out=out, in_=in_), rd, wr)
        else:
            self.P.op(eng, lambda e: e.tensor_copy(out=out, in_=in_), rd, wr)

    def TT(self, eng, out, in0, in1, op, rd, wr):
        self.P.op(eng, lambda e: e.tensor_tensor(out=out, in0=in0, in1=in1, op=op), rd, wr)

    def TS(self, eng, out, in0, s1, s2, op0, op1, rd, wr):
        if s2 is None:
            self.P.op(eng, lambda e: e.tensor_scalar(out=out, in0=in0, scalar1=s1, scalar2=None, op0=op0), rd, wr)
        else:
            self.P.op(eng, lambda e: e.tensor_scalar(out=out, in0=in0, scalar1=s1, scalar2=s2, op0=op0, op1=op1), rd, wr)

    def STT(self, eng, out, in0, scalar, in1, op0, op1, rd, wr):
        self.P.op(eng, lambda e: e.scalar_tensor_tensor(out=out, in0=in0, scalar=scalar, in1=in1, op0=op0, op1=op1), rd, wr)

    def MEMSET(self, eng, ap, val, wr):
        self.P.op(eng, lambda e: e.memset(ap, val), [], wr)

    def DMA(self, out, in_, rd=(), wr=()):
        shp = tuple(out.shape)
        if len(shp) == 3 and shp[0] * shp[1] > DMA_MAXD:
            step = max(1, DMA_MAXD // shp[0])
            for a in range(0, shp[1], step):
                b = min(shp[1], a + step)
                self.P.dma(out[:, a:b, :], in_[:, a:b, :], rd, wr)
            return
        self.P.dma(out, in_, rd, wr)

    def LN(self, xv, F, g, b, out, xkey, okey, gkeys):
        n = self.lncnt = getattr(self, 'lncnt', 0) + 1
        s = n % 2
        st = self.ln_st[s]; mv = self.ln_mv[s]
        nch = F // 512
        for k in range(nch):
            self.P.op('dve', lambda e, k=k: e.bn_stats(out=st[:, k * 6:(k + 1) * 6], in_=xv[:, k * 512:(k + 1) * 512]),
                      [xkey], [('lnst', s, k)])
        self.P.op('dve', lambda e: e.bn_aggr(out=mv[:, 0:2], in_=st[:, 0:nch * 6].rearrange("p (a b) -> p a b", b=6)),
                  [('lnst', s, k) for k in range(nch)], [('lnmv', s)])
        self.ACT(mv[:, 2:3], mv[:, 1:2], AF.Sqrt, [('lnmv', s), 'eps'], [('lnrs', s)], bias=self.eps_t[:, 0:1])
        self.P.op('dve', lambda e: e.reciprocal(out=mv[:, 3:4], in_=mv[:, 2:3]), [('lnrs', s)], [('lnrr', s)])
        self.TS('dve', xv, xv, mv[:, 0:1], mv[:, 3:4], ALU.subtract, ALU.mult, [xkey, ('lnmv', s), ('lnrr', s)], [xkey])
        self.TT('pool', xv, xv, g, ALU.mult, [xkey] + gkeys, [xkey])
        self.TT('pool', out, xv, b, ALU.add, [xkey] + gkeys, [okey] if okey != xkey else [xkey])

    def ln_setup(self):
        self.ln_st = [self.f32(24), self.f32(24)]
        self.ln_mv = [self.f32(4), self.f32(4)]
        self.eps_t = self.f32(2)
        self.MEMSET('pool', self.eps_t, EPS, ['eps'])

    def gb_setup(self):
        self.gbuf = [self.f32(512) for _ in range(2)]
        self.bbuf = [self.f32(512) for _ in range(2)]
        self.gbcnt = 0

    def LNc(self, xv, F, gsrc, bsrc, xkey):
        n = self.lncnt = getattr(self, 'lncnt', 0) + 1
        s = n % 2
        st = self.ln_st[s]; mv = self.ln_mv[s]
        nch = F // 512
        for k in range(nch):
            self.P.op('dve', lambda e, k=k: e.bn_stats(out=st[:, k * 6:(k + 1) * 6], in_=xv[:, k * 512:(k + 1) * 512]),
                      [xkey], [('lnst', s, k)])
        self.P.op('dve', lambda e: e.bn_aggr(out=mv[:, 0:2], in_=st[:, 0:nch * 6].rearrange("p (a b) -> p a b", b=6)),
                  [('lnst', s, k) for k in range(nch)], [('lnmv', s)])
        self.ACT(mv[:, 2:3], mv[:, 1:2], AF.Sqrt, [('lnmv', s), 'eps'], [('lnrs', s)], bias=self.eps_t[:, 0:1])
        self.P.op('dve', lambda e: e.reciprocal(out=mv[:, 3:4], in_=mv[:, 2:3]), [('lnrs', s)], [('lnrr', s)])
        self.TS('dve', xv, xv, mv[:, 0:1], mv[:, 3:4], ALU.subtract, ALU.mult, [xkey, ('lnmv', s), ('lnrr', s)], [xkey])
        for k in range(nch):
            self.gbcnt += 1
            m = self.gbcnt % 2
            self.DMA(self.gbuf[m], gsrc[:, k * 512:(k + 1) * 512], [], [('gbuf', m)])
            self.DMA(self.bbuf[m], bsrc[:, k * 512:(k + 1) * 512], [], [('bbuf', m)])
            xc = xv[:, k * 512:(k + 1) * 512]
            self.TT('dve', xc, xc, self.gbuf[m], ALU.mult, [xkey, ('gbuf', m)], [xkey])
            self.TT('pool', xc, xc, self.bbuf[m], ALU.add, [xkey, ('bbuf', m)], [xkey])

    def wstream_init(self, nslots, pieces, pre):
        self.w_slots = [self.bf(8192) for _ in range(nslots)]
        self.w_pieces = pieces
        self.w_issued = 0
        self.w_pre = pre

    def wget(self, n):
        ns = len(self.w_slots)
        while self.w_issued < len(self.w_pieces) and self.w_issued <= n + self.w_pre:
            m = self.w_issued
            src, kc, cols = self.w_pieces[m]
            view = self.w_slots[m % ns][:, 0:kc * cols].rearrange("p (k c) -> p k c", c=cols)
            self.DMA(view, src, [], [('w', m % ns)])
            self.w_issued += 1
        src, kc, cols = self.w_pieces[n]
        view = self.w_slots[n % ns][:, 0:kc * cols].rearrange("p (k c) -> p k c", c=cols)
        return view, ('w', n % ns)

    def build(self):
        nc = self.nc
        x = self.din("x", [NT, D])
        w_in = self.din("w_in", [L, D, DIN])
        w_pool = self.din("w_pool", [L, 1024, 256])
        w_brp = self.din("w_brp", [L, 1024, D])
        w_brs = self.din("w_brs", [L, 1024, D])
        w_brn = self.din("w_brn", [L, 1024, D])
        w_out = self.din("w_out", [L, D, D])
        w_up = self.din("w_up", [L, D, DFF])
        w_down = self.din("w_down", [L, DFF, D])
        spool = self.din("spool", [L, 128, 8])
        sgug = self.din("sgug", [L, 128, 1024])
        sgub = self.din("sgub", [L, 128, 1024])
        wsT = self.din("wsT", [L, 128, 1024])
        bs = self.din("bs", [L, 1, 1024])
        b2n = self.din("b2n", [L, 16, 128, 1024])
        rm = self.din("rm", [128, 1024])
        sm = self.din("sm", [128, 2 * 1536])
        midflag = self.din("midflag", [128, 1])
        pfix = self.din("pfix", [128, 128])
        ln1g = self.din("ln1g", [L, 128, D])
        ln1b = self.din("ln1b", [L, 128, D])
        ln2g = self.din("ln2g", [L, 128, D])
        ln2b = self.din("ln2b", [L, 128, D])
        ident = self.din("ident", [128, 128])
        y = nc.dram_tensor("y", [NT, D], F32, kind="ExternalOutput").ap()

        S = self.dscr
        wb_in = [S(f"wb_in{l}", [D, DIN], BF16) for l in range(L)]
        wb_pool = [S(f"wb_pool{l}", [1024, 256], BF16) for l in range(L)]
        wb_brp = [S(f"wb_brp{l}", [1024, D], BF16) for l in range(L)]
        wb_brs = [S(f"wb_brs{l}", [1024, D], BF16) for l in range(L)]
        wb_brn = [S(f"wb_brn{l}", [1024, D], BF16) for l in range(L)]
        wb_out = [S(f"wb_out{l}", [D, D], BF16) for l in range(L)]
        wb_up = [S(f"wb_up{l}", [D, DFF], BF16) for l in range(L)]
        wb_down = [S(f"wb_down{l}", [DFF, D], BF16) for l in range(L)]
        xT_d = S("xT_d", [D, NT], BF16)
        aT_d = S("aT_d", [1024, NT + 16], F32)
        uT_d = S("uT_d", [1024, NT], F32)
        vln_d = S("vln_d", [NT, 1024], BF16)
        qT_d = S("qT_d", [1024, NT], BF16)
        kT_d = S("kT_d", [1024, NT], BF16)
        V_d = S("V_d", [NT, 2048], BF16)
        OT_d = S("OT_d", [1024, NT], BF16)
        ypaT_d = S("ypaT_d", [1024, NT], BF16)
        ypbT_d = S("ypbT_d", [1024, NT], BF16)
        x1_d = S("x1_d", [NT, D], F32)
        x1T_d = S("x1T_d", [D, NT], BF16)
        xmid_d = S("xmid_d", [NT, D], F32)
        En_d = S("En_d", [16, 128, 1024], BF16)
        mrg_d = S("mrg_d", [D, NT], BF16)

        P = self.P
        with nc.sbuf_tensor("arena", [128, ARENA_F32], F32) as arena, \
                nc.psum_tensor("ps", [128, 4096], F32) as ps:
            self.arena = arena
            self.ps = ps

            def done(tag):
                return self.stop_after == tag

            self.areset()
            cf = [self.f32(2048) for _ in range(4)]
            cb = [self.bf(2048) for _ in range(4)]
            zt = self.f32(8)
            pieces = []
            for l in range(L):
                for (src, dst, R, C) in ((w_in[l], wb_in[l], D, DIN), (w_pool[l], wb_pool[l], 1024, 256),
                                         (w_brp[l], wb_brp[l], 1024, D), (w_brs[l], wb_brs[l], 1024, D),
                                         (w_brn[l], wb_brn[l], 1024, D), (w_out[l], wb_out[l], D, D),
                                         (w_up[l], wb_up[l], D, DFF), (w_down[l], wb_down[l], DFF, D)):
                    if self.stop_after is not None and l == 1 and not os.environ.get('K_FULLP0'):
                        continue
                    for rb in range(R // 128):
                        for c0 in range(0, C, 2048):
                            w = min(2048, C - c0)
                            pieces.append((src[rb * 128:(rb + 1) * 128, c0:c0 + w],
                                           dst[rb * 128:(rb + 1) * 128, c0:c0 + w], w))
            LAG = 2
            for n in range(len(pieces) + LAG):
                if n < len(pieces):
                    s = n % 4
                    self.DMA(cf[s][:, 0:pieces[n][2]], pieces[n][0], [], [('cf', s)])
                m = n - LAG
                if m >= 0:
                    s = m % 4
                    w = pieces[m][2]
                    self.CP(self.eng3(), cb[s][:, 0:w], cf[s][:, 0:w], [('cf', s)], [('cb', s)])
                    self.DMA(pieces[m][1], cb[s][:, 0:w], [('cb', s)], [])
            self.MEMSET('pool', zt, 0.0, ['zt'])
            for c in range(8):
                self.DMA(aT_d[c * 128:(c + 1) * 128, 0:8], zt, ['zt'], [])
                self.DMA(aT_d[c * 128:(c + 1) * 128, NT + 8:NT + 16], zt, ['zt'], [])
            P.barrier()

            for l in range(L if not done(('P0', 0)) else 0):
                xsrc = x if l == 0 else xmid_d
                xdst = xmid_d if l == 0 else y
                T = 512
                self.areset()
                self.ln_setup()
                idf = self.f32(128); idb = self.bf(128)
                xt = [self.f32(2048) for _ in range(2)]
                xtb = [self.bf(2048) for _ in range(4)]
                xT = [self.bf(16 * T).rearrange("p (k t) -> p k t", t=T) for _ in range(2)]
                st32 = [self.f32(512) for _ in range(4)]
                stb = [self.bf(512) for _ in range(4)]
                vg = self.f32(4 * 1024).rearrange("p (s c) -> p s c", c=1024)
                vlnb = [self.bf(1024) for _ in range(2)]
                Vst = [self.bf(2048) for _ in range(4)]
                gt = self.f32(1024); bt = self.f32(1024)
                self.DMA(idf, ident, [], ['idf'])
                self.CP('dve', idb, idf, ['idf'], ['idb'])
                self.DMA(gt, sgug[l], [], ['sg_g'])
                self.DMA(bt, sgub[l], [], ['sg_g'])
                for ts in range(4):
                    v3 = Vst[ts].rearrange("p (h c) -> p h c", c=128)
                    self.MEMSET('pool', v3[:, :, 64:128], 1.0, [('Vst1', ts)])
                wsrc = wb_in[l].rearrange("(k p) c -> p k c", p=128)
                tl = self.tiles(NT // T)
                jset = [int(v) for v in os.environ.get("K_JSET", "0,1,2,3,4,5,6,7,8,9,10,11").split(",")]
                pieces = []
                for i in tl:
                    for j in range(12):
                        pieces.append((wsrc[:, :, j * 512:(j + 1) * 512], 16, 512))
                self.wstream_init(3, pieces, 2)
                aT_v = aT_d.rearrange("(c p) t -> p c t", p=128)
                pn = 0
                cnt = 0
                for ii, i in enumerate(tl):
                    t0 = i * T
                    sl = ii % 2
                    for ts in range(4):
                        self.DMA(xt[ts % 2], xsrc[t0 + ts * 128:t0 + (ts + 1) * 128, :], [], [('xt', ts % 2)])
                        self.CP(('dve', 'pool')[ts % 2], xtb[ts], xt[ts % 2], [('xt', ts % 2)], [('xtb', ts)])
                    for kc in range(16):
                        bk, bkeys = self.nb()
                        pb = bk.bitcast(BF16)
                        for ts in range(4):
                            self.TR(pb[:, ts * 128:(ts + 1) * 128], xtb[ts][:, kc * 128:(kc + 1) * 128], idb,
                                    [('xtb', ts), 'idb'], bkeys)
                        self.CP(self.eng2(), xT[sl][:, kc, :], pb[:, 0:512], bkeys, [('xT', sl, kc)])
                    xkeys = [('xT', sl, kc) for kc in range(16)]
                    self.DMA(xT_d.rearrange("(k p) t -> p k t", p=128)[:, :, t0:t0 + T], xT[sl], xkeys, [])
                    for j in range(12):
                        wv, wkey = self.wget(pn); pn += 1
                        if j not in jset:
                            continue
                        if j in (0, 1, 2, 3, 6, 7, 8, 9):
                            for cc in range(4):
                                bk, bkeys = self.nbfull()
                                for kc in range(16):
                                    self.MM(bk, wv[:, kc, cc * 128:(cc + 1) * 128], xT[sl][:, kc, :], kc == 0, kc == 15,
                                            [wkey, ('xT', sl, kc)], bkeys)
                                cnt += 1
                                k = cnt % 4
                                if j < 2:
                                    ch = j * 4 + cc
                                    self.CP(self.eng2(), st32[k], bk, bkeys, [('st32', k)])
                                    self.DMA(aT_d[ch * 128:(ch + 1) * 128, 8 + t0:8 + t0 + T], st32[k], [('st32', k)], [])
                                elif j < 4:
                                    ch = (j - 2) * 4 + cc
                                    self.ACT(st32[k], bk, AF.Gelu_apprx_tanh, bkeys, [('st32', k)])
                                    self.DMA(uT_d[ch * 128:(ch + 1) * 128, t0:t0 + T], st32[k], [('st32', k)], [])
                                else:
                                    dst = qT_d if j < 8 else kT_d
                                    ch = ((j - 6) % 2) * 4 + cc
                                    self.CP(self.eng2(), stb[k], bk, bkeys, [('stb', k)])
                                    self.DMA(dst[ch * 128:(ch + 1) * 128, t0:t0 + T], stb[k], [('stb', k)], [])
                        elif j in (4, 5):
                            for ts in range(4):
                                bk, bkeys = self.nbfull()
                                for kc in range(16):
                                    self.MM(bk, xT[sl][:, kc, ts * 128:(ts + 1) * 128], wv[:, kc, :], kc == 0, kc == 15,
                                            [wkey, ('xT', sl, kc)], bkeys)
                                self.ACT(vg[:, ts, (j - 4) * 512:(j - 3) * 512], bk, AF.Gelu_apprx_tanh, bkeys,
                                         [('vg', ts)])
                            if j == 5:
                                for ts in range(4):
                                    o = vlnb[ts % 2]
                                    self.LN(vg[:, ts, :], 1024, gt, bt, o, ('vg', ts), ('vlnb', ts % 2), ['sg_g'])
                                    self.DMA(vln_d[t0 + ts * 128:t0 + (ts + 1) * 128, :], o, [('vlnb', ts % 2)], [])
                        else:
                            for ts in range(4):
                                bk, bkeys = self.nbfull()
                                for kc in range(16):
                                    self.MM(bk, xT[sl][:, kc, ts * 128:(ts + 1) * 128], wv[:, kc, :], kc == 0, kc == 15,
                                            [wkey, ('xT', sl, kc)], bkeys)
                                v3 = Vst[ts].rearrange("p (h c) -> p h c", c=128)
                                self.CP(self.eng2(), v3[:, (j - 10) * 8:(j - 9) * 8, 0:64],
                                        bk.rearrange("p (h c) -> p h c", c=64), bkeys, [('Vst', ts, j - 10)])
                            if j == 11:
                                for ts in range(4):
                                    self.DMA(V_d[t0 + ts * 128:t0 + (ts + 1) * 128, :], Vst[ts],
                                             [('Vst', ts, 0), ('Vst', ts, 1), ('Vst1', ts)], [('Vst', ts, 0), ('Vst', ts, 1)])
                P.barrier()
                if done(('P1', l)):
                    break

                self.areset()
                Em = self.bf(16 * 1024).rearrange("p (h c) -> p h c", c=1024)
                smb = self.bf(2 * 1536)
                mark = self.aoff
                btmp = [self.f32(1024) for _ in range(2)]
                en32 = [self.f32(1024) for _ in range(2)]
                rmt = self.f32(1024)
                enb0 = [self.bf(1024) for _ in range(2)]
                smf = self.f32(2 * 1536)
                self.DMA(rmt, rm, [], ['rm'])
                self.DMA(smf, sm, [], ['smf'])
                self.CP('dve', smb, smf, ['smf'], ['smb'])
                for h in range(16):
                    s = h % 2
                    self.DMA(btmp[s], b2n[l, h], [], [('btmp', s)])
                    self.ACT(en32[s], btmp[s], AF.Exp, [('btmp', s)], [('en32', s)])
                    self.TT('pool', Em[:, h, :], en32[s], rmt, ALU.mult, [('en32', s), 'rm'], [('Em', h)])
                    self.CP('dve', enb0[s], en32[s], [('en32', s)], [('enb0', s)])
                    self.DMA(En_d[h], enb0[s], [('enb0', s)], [])
                P.barrier()
                self.aoff = mark
                T = 512
                qTt = [self.bf(8 * T).rearrange("p (k t) -> p k t", t=T) for _ in range(2)]
                kTw = [self.bf(8 * 1024).rearrange("p (k t) -> p k t", t=1024) for _ in range(2)]
                Vr = [self.bf(2048) for _ in range(12)]
                OTt = [self.bf(8 * T).rearrange("p (k t) -> p k t", t=T) for _ in range(2)]
                eS = [self.f32(1536) for _ in range(2)]
                Pm = [self.bf(1536) for _ in range(2)]
                enb = [self.bf(1024) for _ in range(2)]
                rd = [self.f32(256) for _ in range(2)]
                qT_v = qT_d.rearrange("(k p) t -> p k t", p=128)
                kT_v = kT_d.rearrange("(k p) t -> p k t", p=128)
                OT_v = OT_d.rearrange("(k p) t -> p k t", p=128)
                tl = self.tiles(NT // T)
                consec = (tl == list(range(NT // T)))

                def na_loads(ii):
                    i = tl[ii]
                    sl = ii % 2
                    t0 = i * T
                    self.DMA(qTt[sl], qT_v[:, :, t0:t0 + T], [], [('qTt', sl)])
                    lo = max(0, 4 * i - 2); hi = min(63, 4 * i + 5)
                    o0 = (lo - (4 * i - 2)) * 128
                    self.DMA(kTw[sl][:, :, o0:o0 + (hi - lo + 1) * 128], kT_v[:, :, lo * 128:(hi + 1) * 128], [], [('kTw', sl)])
                    if consec:
                        cs = range(0, 6) if i == 0 else range(4 * i + 2, min(63, 4 * i + 5) + 1)
                    else:
                        cs = range(lo, hi + 1)
                    for c in cs:
                        self.DMA(Vr[c % 12], V_d[c * 128:(c + 1) * 128, :], [], [('Vr', c % 12)])

                na_loads(0)
                cntn = 0
                encnt = 0
                for ii, i in enumerate(tl):
                    sl = ii % 2
                    t0 = i * T
                    if ii + 1 < len(tl):
                        na_loads(ii + 1)
                    for b in range(2):
                        B = 2 * i + b
                        special = B in (15, 16)
                        nomask = B in (0, 15, 16, 31)
                        sval = [s for s in range(6) if 0 <= 2 * B + (3 - s) <= 63]
                        s0, s1 = sval[0], sval[-1] + 1
                        ns = s1 - s0
                        for h in range(16):
                            po = (h % 2) * 64
                            hp = h // 2
                            n = cntn % 2
                            cntn += 1
                            psS = self.ps[:, n * 1536:(n + 1) * 1536]
                            psO = self.ps[:, 3072 + n * 512:3072 + n * 512 + 256]
                            for s in sval:
                                c = 2 * B + 3 - s
                                sk = c - (4 * i - 2)
                                self.MM(psS[:, s * 256:(s + 1) * 256], kTw[sl][po:po + 64, hp, sk * 128:(sk + 1) * 128],
                                        qTt[sl][po:po + 64, hp, b * 256:(b + 1) * 256], True, True,
                                        [('kTw', sl), ('qTt', sl)], [('psS', n)])
                            self.ACT(eS[n][:, s0 * 256:s1 * 256], psS[:, s0 * 256:s1 * 256], AF.Exp, [('psS', n)], [('eS', n)],
                                     scale=0.125)
                            if nomask:
                                m = encnt % 2
                                encnt += 1
                                self.DMA(enb[m], En_d[h], [], [('enb', m)])
                                tab = enb[m]
                                tkey = ('enb', m)
                            else:
                                tab = Em[:, h, :]
                                tkey = ('Em', h)
                            tap = bass.AP(tab.tensor, tab.offset + 64 + 128 * s0, [list(tab.ap[0]), [128, ns], [1, 256]])
                            pm3 = Pm[n][:, s0 * 256:s1 * 256].rearrange("p (s q) -> p s q", q=256)
                            es3 = eS[n][:, s0 * 256:s1 * 256].rearrange("p (s q) -> p s q", q=256)
                            self.TT('pool', pm3, es3, tap, ALU.mult, [('eS', n), tkey], [('Pm', n)])
                            if special:
                                o = (B - 15) * 1536
                                self.TT('pool', Pm[n][:, s0 * 256:s1 * 256], Pm[n][:, s0 * 256:s1 * 256],
                                        smb[:, o + s0 * 256:o + s1 * 256], ALU.mult, [('Pm', n), 'smb'], [('Pm', n)])
                            for s in sval:
                                c = 2 * B + 3 - s
                                self.MM(psO, Vr[c % 12][:, h * 128:(h + 1) * 128], Pm[n][:, s * 256:(s + 1) * 256],
                                        s == s0, s == s1 - 1, [('Vr', c % 12), ('Pm', n)], [('psO', n)])
                            self.P.op('dve', lambda e, psO=psO, r=rd[n]: e.reciprocal(out=r[0:64, :], in_=psO[64:128, :]),
                                      [('psO', n)], [('rd', n)])
                            self.TT('dve', OTt[sl][po:po + 64, hp, b * 256:(b + 1) * 256], psO[0:64, :], rd[n][0:64, :],
                                    ALU.mult, [('psO', n), ('rd', n)], [('OTt', sl)])
                    self.DMA(OT_v[:, :, t0:t0 + T], OTt[sl], [('OTt', sl)], [])
                P.barrier()
                if done(('P2a', l)):
                    break

                self.areset()
                T = 512
                A_ = [self.f32(8 * 528).rearrange("p (c t) -> p c t", t=528) for _ in range(2)]
                T1 = self.f32(2 * 528).rearrange("p (c t) -> p c t", t=528)
                T2 = self.f32(2 * 528).rearrange("p (c t) -> p c t", t=528)
                T3 = self.f32(2 * 528).rearrange("p (c t) -> p c t", t=528)
                U1 = self.f32(2 * 528).rearrange("p (c t) -> p c t", t=528)
                U2 = self.f32(2 * 528).rearrange("p (c t) -> p c t", t=528)
                U3 = self.f32(2 * 528).rearrange("p (c t) -> p c t", t=528)
                tmp8 = self.f32(16).rearrange("p (c t) -> p c t", t=8)
                pT = self.bf(8 * T).rearrange("p (c t) -> p c t", t=T)
                ypa = [self.bf(8 * T).rearrange("p (c t) -> p c t", t=T) for _ in range(2)]
                uT = [self.f32(8 * T).rearrange("p (c t) -> p c t", t=T) for _ in range(2)]
                vln = [self.bf(4 * 1024).rearrange("p (s c) -> p s c", c=1024) for _ in range(2)]
                ypb = [self.bf(8 * T).rearrange("p (c t) -> p c t", t=T) for _ in range(2)]
                wpl = self.bf(8 * 256).rearrange("p (g k c) -> p g k c", g=4, k=2)
                spl = self.f32(8)
                wsf = self.f32(1024); wsb = self.bf(1024).rearrange("p (g q) -> p g q", q=128)
                bsf = self.f32(1024); ones = self.f32(128)
                mfl = self.f32(2); pfx = self.f32(128)
                self.DMA(wpl, wb_pool[l].rearrange("(g k p) c -> p g k c", g=4, k=2), [], ['wpl'])
                self.DMA(spl, spool[l], [], ['spl'])
                self.DMA(wsf, wsT[l], [], ['wsf'])
                self.CP('dve', wsb.rearrange("p g q -> p (g q)"), wsf, ['wsf'], ['wsb'])
                self.DMA(bsf[0:1, :], bs[l], [], ['bsf'])
                self.MEMSET('pool', ones, 1.0, ['ones'])
                self.DMA(mfl[:, 0:1], midflag, [], ['mfl'])
                self.DMA(pfx, pfix, [], ['pfx'])
                aT_v = aT_d.rearrange("(c p) t -> p c t", p=128)
                uT_v = uT_d.rearrange("(c p) t -> p c t", p=128)
                vln_v = vln_d.rearrange("(n p) c -> p n c", p=128)
                ypa_v = ypaT_d.rearrange("(c p) t -> p c t", p=128)
                ypb_v = ypbT_d.rearrange("(c p) t -> p c t", p=128)
                tl = self.tiles(NT // T)

                def pb_loads(ii):
                    i = tl[ii]
                    sl = ii % 2
                    t0 = i * T
                    self.DMA(A_[sl], aT_v[:, :, t0:t0 + 528], [], [('A', sl, g) for g in range(4)])
                    self.DMA(uT[sl], uT_v[:, :, t0:t0 + T], [], [('uT', sl)])
                    self.DMA(vln[sl], vln_v[:, 4 * i:4 * i + 4, :], [], [('vln', sl)])
                pb_loads(0)
                for ii, i in enumerate(tl):
                    sl = ii % 2
                    t0 = i * T
                    if ii + 1 < len(tl):
                        pb_loads(ii + 1)
                    Akeys = [('A', sl, g) for g in range(4)]
                    if i == 7:
                        self.TS('pool', A_[sl][:, :, 520:528], A_[sl][:, :, 520:528], mfl[:, 0:1], None, ALU.mult, None,
                                Akeys + ['mfl'], Akeys)
                    if i == 8:
                        self.TS('pool', A_[sl][:, :, 0:8], A_[sl][:, :, 0:8], mfl[:, 0:1], None, ALU.mult, None,
                                Akeys + ['mfl'], Akeys)
                    bidx = {0: (0, 0), 7: (1, 504), 8: (2, 0), 15: (3, 504)}.get(i)
                    for g in range(4):
                        eng = ('pool', 'dve')[g % 2]
                        X1, X2, X3 = (T1, T2, T3) if g % 2 == 0 else (U1, U2, U3)
                        kx = 'T' if g % 2 == 0 else 'U'
                        Av = A_[sl][:, 2 * g:2 * g + 2, :]
                        ak = [('A', sl, g)]
                        if g == 0:
                            self.TT(eng, X1[:, :, 0:512], Av[:, :, 7:519], Av[:, :, 8:520], ALU.add, ak, [(kx, 1)])
                            Wv = X1; wk = (kx, 1)
                        elif g == 1:
                            self.TT(eng, X1[:, :, 0:527], Av[:, :, 0:527], Av[:, :, 1:528], ALU.add, ak, [(kx, 1)])
                            self.TT(eng, X2[:, :, 0:512], X1[:, :, 6:518], X1[:, :, 8:520], ALU.add, [(kx, 1)], [(kx, 2)])
                            Wv = X2; wk = (kx, 2)
                        elif g == 2:
                            self.TT(eng, X1[:, :, 0:527], Av[:, :, 0:527], Av[:, :, 1:528], ALU.add, ak, [(kx, 1)])
                            self.TT(eng, X2[:, :, 0:525], X1[:, :, 0:525], X1[:, :, 2:527], ALU.add, [(kx, 1)], [(kx, 2)])
                            self.TT(eng, X3[:, :, 0:512], X2[:, :, 4:516], X2[:, :, 8:520], ALU.add, [(kx, 2)], [(kx, 3)])
                            Wv = X3; wk = (kx, 3)
                        else:
                            self.TT(eng, X1[:, :, 0:527], Av[:, :, 0:527], Av[:, :, 1:528], ALU.add, ak, [(kx, 1)])
                            self.TT(eng, X2[:, :, 0:525], X1[:, :, 0:525], X1[:, :, 2:527], ALU.add, [(kx, 1)], [(kx, 2)])
                            self.TT(eng, X3[:, :, 0:521], X2[:, :, 0:521], X2[:, :, 4:525], ALU.add, [(kx, 2)], [(kx, 3)])
                            self.TT(eng, X1[:, :, 0:512], X3[:, :, 0:512], X3[:, :, 8:520], ALU.add, [(kx, 3)], [(kx, 1)])
                            Wv = X1; wk = (kx, 1)
                        wdt = 2 ** (g + 1)
                        self.STT('dve', pT[:, 2 * g:2 * g + 2, :], Wv[:, :, 0:512], 1.0 / wdt, Av[:, :, 8:520],
                                 ALU.mult, ALU.subtract, [wk] + ak, [('pT', g)])
                        if bidx is not None:
                            bi, c0 = bidx
                            pf = bass.AP(pfx.tensor, pfx.offset + bi * 32 + g * 8, [list(pfx.ap[0]), [0, 2], [1, 8]])
                            self.TT(eng, tmp8, Wv[:, :, c0:c0 + 8], pf, ALU.mult, [wk, 'pfx'], ['tmp8'])
                            self.TT(eng, pT[:, 2 * g:2 * g + 2, c0:c0 + 8], tmp8, Av[:, :, 8 + c0:16 + c0], ALU.subtract,
                                    ['tmp8', ('pT', g)] + ak, [('pT', g)])
                    for g in range(4):
                        for oc in range(2):
                            bk, bkeys = self.nbfull()
                            for kc in range(2):
                                self.MM(bk, wpl[:, g, kc, oc * 128:(oc + 1) * 128], pT[:, 2 * g + kc, :], kc == 0, kc == 1,
                                        ['wpl', ('pT', g)], bkeys)
                            self.ACT(ypa[sl][:, 2 * g + oc, :], bk, AF.Copy, bkeys + ['spl'], [('ypa', sl)],
                                     scale=spl[:, 2 * g + oc:2 * g + oc + 1])
                    self.DMA(ypa_v[:, :, t0:t0 + T], ypa[sl], [('ypa', sl)], [])
                    for g in range(8):
                        bk, bkeys = self.nbfull()
                        for ts in range(4):
                            self.MM(bk[:, ts * 128:(ts + 1) * 128], vln[sl][:, ts, g * 128:(g + 1) * 128], wsb[:, g, :],
                                    True, False, [('vln', sl), 'wsb'], bkeys)
                            self.MM(bk[:, ts * 128:(ts + 1) * 128], ones[0:1, 0:128], bsf[0:1, g * 128:(g + 1) * 128],
                                    False, True, ['ones', 'bsf'], bkeys)
                        self.TT('dve', ypb[sl][:, g, :], bk, uT[sl][:, g, :], ALU.mult, bkeys + [('uT', sl)], [('ypb', sl)])
                    self.DMA(ypb_v[:, :, t0:t0 + T], ypb[sl], [('ypb', sl)], [])
                P.barrier()
                if done(('P2b', l)):
                    break

                self.areset()
                T = 512
                tl = self.tiles(NT // T)
                xTt = [self.bf(16 * T).rearrange("p (k t) -> p k t", t=T) for _ in range(2)]
                ypat = self.bf(8 * T).rearrange("p (k t) -> p k t", t=T)
                ypbt = self.bf(8 * T).rearrange("p (k t) -> p k t", t=T)
                OTt3 = self.bf(8 * T).rearrange("p (k t) -> p k t", t=T)
                sg = [self.f32(4 * T).rearrange("p (k t) -> p k t", t=T) for _ in range(2)]
                tt = [self.f32(T) for _ in range(2)]
                acc = self.f32(4 * T).rearrange("p (k t) -> p k t", t=T)
                mrg = [self.bf(16 * T).rearrange("p (k t) -> p k t", t=T) for _ in range(2)]
                win_v = wb_in[l].rearrange("(k p) c -> p k c", p=128)
                brv = [w.rearrange("(k p) c -> p k c", p=128) for w in (wb_brp[l], wb_brs[l], wb_brn[l])]
                pieces = []
                for i in tl:
                    for dcg in range(4):
                        for br in range(3):
                            c0 = 6144 + br * 2048 + dcg * 512
                            pieces.append((win_v[:, :, c0:c0 + 512], 16, 512))
                            pieces.append((brv[br][:, :, dcg * 512:(dcg + 1) * 512], 8, 512))
                self.wstream_init(4, pieces, 3)
                xT_v = xT_d.rearrange("(k p) t -> p k t", p=128)
                OT_v = OT_d.rearrange("(k p) t -> p k t", p=128)
                mrg_v = mrg_d.rearrange("(k p) t -> p k t", p=128)
                brin = (ypat, ypbt, OTt3)
                brsrc = (ypa_v, ypb_v, OT_v)

                def p3_loads(ii):
                    i = tl[ii]
                    t0 = i * T
                    self.DMA(xTt[ii % 2], xT_v[:, :, t0:t0 + T], [], [('xTt', ii % 2)])
                    for br in range(3):
                        self.DMA(brin[br], brsrc[br][:, :, t0:t0 + T], [], [('brin', br)])
                p3_loads(0)
                pn = 0
                c2 = 0
                for ii, i in enumerate(tl):
                    sl = ii % 2
                    t0 = i * T
                    for dcg in range(4):
                        for br in range(3):
                            wv, wkey = self.wget(pn); pn += 1
                            sgi = (dcg * 3 + br) % 2
                            for dc4 in range(4):
                                bk, bkeys = self.nbfull()
                                for kc in range(16):
                                    self.MM(bk, wv[:, kc, dc4 * 128:(dc4 + 1) * 128], xTt[sl][:, kc, :], kc == 0, kc == 15,
                                            [wkey, ('xTt', sl)], bkeys)
                                self.ACT(sg[sgi][:, dc4, :], bk, AF.Sigmoid, bkeys, [('sg', sgi, dc4)])
                            wv, wkey = self.wget(pn); pn += 1
                            for dc4 in range(4):
                                bk, bkeys = self.nbfull()
                                for kc in range(8):
                                    self.MM(bk, wv[:, kc, dc4 * 128:(dc4 + 1) * 128], brin[br][:, kc, :], kc == 0, kc == 7,
                                            [wkey, ('brin', br)], bkeys)
                                if br == 0:
                                    self.TT('dve', acc[:, dc4, :], bk, sg[sgi][:, dc4, :], ALU.mult,
                                            bkeys + [('sg', sgi, dc4)], [('acc', dc4)])
                                else:
                                    c2 += 1
                                    m = c2 % 2
                                    self.TT('dve', tt[m], bk, sg[sgi][:, dc4, :], ALU.mult,
                                            bkeys + [('sg', sgi, dc4)], [('tt', m)])
                                    if br == 1:
                                        self.TT('pool', acc[:, dc4, :], acc[:, dc4, :], tt[m], ALU.add,
                                                [('acc', dc4), ('tt', m)], [('acc', dc4)])
                                    else:
                                        self.TT('pool', mrg[sl][:, dcg * 4 + dc4, :], acc[:, dc4, :], tt[m], ALU.add,
                                                [('acc', dc4), ('tt', m)], [('mrg', sl)])
                    self.DMA(mrg_v[:, :, t0:t0 + T], mrg[sl], [('mrg', sl)], [])
                    if ii + 1 < len(tl):
                        p3_loads(ii + 1)
                P.barrier()
                if done(('P3a', l)):
                    break

                self.areset()
                self.ln_setup()
                T = 512
                tl = self.tiles(NT // T)
                idf = self.f32(128); idb = self.bf(128)
                self.DMA(idf, ident, [], ['idf'])
                self.CP('dve', idb, idf, ['idf'], ['idb'])
                self.gb_setup()
                mrgt = [self.bf(16 * T).rearrange("p (k t) -> p k t", t=T) for _ in range(2)]
                xtk = self.f32(4 * D).rearrange("p (s c) -> p s c", c=D)
                mT = [self.f32(4 * T).rearrange("p (k t) -> p k t", t=T) for _ in range(2)]
                x1b = [self.bf(D) for _ in range(4)]
                x1T = self.bf(16 * T).rearrange("p (k t) -> p k t", t=T)
                wout_v = wb_out[l].rearrange("(k p) c -> p k c", p=128)
                pieces = []
                for i in tl:
                    for cg in range(4):
                        pieces.append((wout_v[:, :, cg * 512:(cg + 1) * 512], 16, 512))
                self.wstream_init(3, pieces, 2)
                x1T_v = x1T_d.rearrange("(k p) t -> p k t", p=128)
                self.DMA(mrgt[0], mrg_v[:, :, tl[0] * T:tl[0] * T + T], [], [('mrgt', 0)])
                pn = 0

                def p3b_mm(cg, sl):
                    nonlocal pn
                    wv, wkey = self.wget(pn); pn += 1
                    mi = cg % 2
                    for dc4 in range(4):
                        bk, bkeys = self.nbfull()
                        for kc in range(16):
                            self.MM(bk, wv[:, kc, dc4 * 128:(dc4 + 1) * 128], mrgt[sl][:, kc, :], kc == 0, kc == 15,
                                    [wkey, ('mrgt', sl)], bkeys)
                        self.CP('act', mT[mi][:, dc4, :], bk, bkeys, [('mT', mi, dc4)])

                def p3b_tr(cg):
                    mi = cg % 2
                    for ts in range(4):
                        bk, bkeys = self.nbfull()
                        for dc4 in range(4):
                            self.TR(bk[:, dc4 * 128:(dc4 + 1) * 128], mT[mi][:, dc4, ts * 128:(ts + 1) * 128], idf,
                                    [('mT', mi, dc4), 'idf'], bkeys)
                        xv = xtk[:, ts, cg * 512:(cg + 1) * 512]
                        self.STT('dve', xv, xv, ALPHA, bk, ALU.mult, ALU.add, bkeys + [('xtk', ts)], [('xtk', ts)])

                def p3b_tail_nonpe(t0):
                    for ts in range(4):
                        self.LNc(xtk[:, ts, :], D, ln1g[l], ln1b[l], ('xtk', ts))
                        self.DMA(x1_d[t0 + ts * 128:t0 + (ts + 1) * 128, :], xtk[:, ts, :], [('xtk', ts)], [])
                        self.CP('act', x1b[ts], xtk[:, ts, :], [('xtk', ts)], [('x1b', ts)])

                def p3b_tail_pe(t0):
                    for ts in range(4):
                        for kq in range(4):
                            bk, bkeys = self.nb()
                            pb = bk.bitcast(BF16)
                            for k4 in range(4):
                                kc = kq * 4 + k4
                                self.TR(pb[:, k4 * 128:(k4 + 1) * 128], x1b[ts][:, kc * 128:(kc + 1) * 128], idb,
                                        [('x1b', ts), 'idb'], bkeys)
                            self.CP(self.eng2(), x1T[:, kq * 4:kq * 4 + 4, ts * 128:(ts + 1) * 128],
                                    pb.rearrange("p (k t) -> p k t", t=128), bkeys, [('x1T', ts, kq)])
                    self.DMA(x1T_v[:, :, t0:t0 + T], x1T, [('x1T', ts, kq) for ts in range(4) for kq in range(4)], [])

                for ii, i in enumerate(tl):
                    sl = ii % 2
                    t0 = i * T
                    if ii + 1 < len(tl):
                        tn = tl[ii + 1] * T
                        self.DMA(mrgt[1 - sl], mrg_v[:, :, tn:tn + T], [], [('mrgt', 1 - sl)])
                    p3b_mm(0, sl)
                    p3b_mm(1, sl)
                    if ii > 0:
                        p3b_tail_pe(tl[ii - 1] * T)
                    self.DMA(xtk, xsrc.rearrange("(n p) c -> p n c", p=128)[:, 4 * i:4 * i + 4, :], [],
                             [('xtk', ts) for ts in range(4)])
                    p3b_tr(0)
                    p3b_tr(1)
                    p3b_mm(2, sl)
                    p3b_tr(2)
                    p3b_mm(3, sl)
                    p3b_tr(3)
                    p3b_tail_nonpe(t0)
                p3b_tail_pe(tl[-1] * T)
                P.barrier()
                if done(('P3', l)):
                    break

                self.areset()
                self.ln_setup()
                T = 512
                tl = self.tiles(NT // T)
                idf = self.f32(128)
                self.DMA(idf, ident, [], ['idf'])
                self.gb_setup()
                x1Tt = self.bf(16 * T).rearrange("p (k t) -> p k t", t=T)
                x1k = self.f32(4 * D).rearrange("p (s c) -> p s c", c=D)
                hT = self.bf(64 * T).rearrange("p (k t) -> p k t", t=T)
                r32 = [self.f32(T) for _ in range(2)]
                fT = self.f32(4 * T).rearrange("p (k t) -> p k t", t=T)
                wup_v = wb_up[l].rearrange("(k p) c -> p k c", p=128)
                wdn_v = wb_down[l].rearrange("(k p) c -> p k c", p=128)
                pieces = []
                for i in tl:
                    for j in range(16):
                        pieces.append((wup_v[:, :, j * 512:(j + 1) * 512], 16, 512))
                    for cg in range(4):
                        for q in range(4):
                            pieces.append((wdn_v[:, 16 * q:16 * q + 16, cg * 512:(cg + 1) * 512], 16, 512))
                self.wstream_init(3, pieces, 2)
                self.DMA(x1Tt, x1T_v[:, :, tl[0] * T:tl[0] * T + T], [], ['x1Tt'])
                pn = 0
                c4 = 0
                for ii, i in enumerate(tl):
                    t0 = i * T
                    for j in range(16):
                        wv, wkey = self.wget(pn); pn += 1
                        for fc4 in range(4):
                            bk, bkeys = self.nbfull()
                            for kc in range(16):
                                self.MM(bk, wv[:, kc, fc4 * 128:(fc4 + 1) * 128], x1Tt[:, kc, :], kc == 0, kc == 15,
                                        [wkey, 'x1Tt'], bkeys)
                            c4 += 1
                            m = c4 % 2
                            self.ACT(r32[m], bk, AF.Relu, bkeys, [('r32', m)])
                            self.TT('pool', hT[:, j * 4 + fc4, :], r32[m], r32[m], ALU.mult, [('r32', m)], [('hT', j * 4 + fc4)])
                    self.DMA(x1k, x1_d.rearrange("(n p) c -> p n c", p=128)[:, 4 * i:4 * i + 4, :], [],
                             [('x1k', ts) for ts in range(4)])
                    if ii + 1 < len(tl):
                        tn = tl[ii + 1] * T
                        self.DMA(x1Tt, x1T_v[:, :, tn:tn + T], [], ['x1Tt'])

                    def down_tr(cg):
                        for ts in range(4):
                            bi = (cg % 2) * 4 + ts
                            bk = self.ps[:, bi * 512:(bi + 1) * 512]
                            bkeys = [('bank', bi)]
                            for dc4 in range(4):
                                self.TR(bk[:, dc4 * 128:(dc4 + 1) * 128], fT[:, dc4, ts * 128:(ts + 1) * 128], idf,
                                        [('fT', dc4), 'idf'], bkeys)
                            xv = x1k[:, ts, cg * 512:(cg + 1) * 512]
                            self.STT('dve', xv, xv, ALPHA, bk, ALU.mult, ALU.add, bkeys + [('x1k', ts)], [('x1k', ts)])

                    for cg in range(4):
                        banks = []
                        for dc4 in range(4):
                            bi = (cg % 2) * 4 + dc4
                            banks.append((self.ps[:, bi * 512:(bi + 1) * 512], [('bank', bi)]))
                        for q in range(4):
                            wv, wkey = self.wget(pn); pn += 1
                            for dc4 in range(4):
                                bk, bkeys = banks[dc4]
                                for kc in range(16):
                                    self.MM(bk, wv[:, kc, dc4 * 128:(dc4 + 1) * 128], hT[:, 16 * q + kc, :],
                                            q == 0 and kc == 0, q == 3 and kc == 15, [wkey, ('hT', 16 * q + kc)], bkeys)
                        if cg > 0:
                            down_tr(cg - 1)
                        for dc4 in range(4):
                            bk, bkeys = banks[dc4]
                            self.CP('act', fT[:, dc4, :], bk, bkeys, [('fT', dc4)])
                    down_tr(3)
                    for ts in range(4):
                        self.LNc(x1k[:, ts, :], D, ln2g[l], ln2b[l], ('x1k', ts))
                        self.DMA(xdst[t0 + ts * 128:t0 + (ts + 1) * 128, :], x1k[:, ts, :], [('x1k', ts)], [])
                P.barrier()
                if done(('P4', l)):
                    break
            P.emit(nc)
        return nc


def _host_tables(is_sample):
    p = np.arange(128)
    kc = p % 64
    up = (p >= 64).astype(np.int64)
    e = np.arange(16)
    qc = np.arange(64)
    dr = (7 - e)[None, :] + up[:, None]
    c0 = np.clip(qc - 8, 0, 48)
    colok = (kc[:, None] >= c0[None, :]) & (kc[:, None] < c0[None, :] + 16)
    drok = np.abs(dr) <= 7
    ridx = np.clip(dr + 7, 0, 14)
    cidx = np.clip(kc[:, None] - qc[None, :] + 15, 0, 30)
    valid = drok[:, :, None] & colok[:, None, :]
    rmask = ((dr >= -4) & (dr <= 3)).astype(np.float32)[:, :, None] * np.ones((1, 1, 64), np.float32)
    sm = np.zeros((128, 2, 6, 4, 64), np.float32)
    for bi, B in enumerate((15, 16)):
        for s in range(6):
            cr = 3 - s
            for r in range(4):
                qr = 4 * B + r
                for u in range(2):
                    kr = 4 * B + 2 * cr + u
                    if is_sample:
                        ok = (kr // 64 == qr // 64) and (0 <= kr < 128)
                        r0 = min(max(qr % 64 - 4, 0), 56)
                        ok = ok and (r0 <= kr % 64 <= r0 + 7)
                    else:
                        r0 = min(max(qr - 4, 0), 120)
                        ok = (r0 <= kr <= r0 + 7)
                    if ok:
                        sm[u * 64:(u + 1) * 64, bi, s, r, :] = 1.0
    pf = np.zeros((4, 4, 8), np.float32)
    for g, w in enumerate((2, 4, 8, 16)):
        half = w // 2
        sl = 4096 if is_sample else 8192
        for j in range(8):
            t = j
            cntL = min(t + half, sl) - max(t - half, 0)
            t = sl - 8 + j
            cntR = min(t + half, sl) - max(t - half, 0)
            pf[0, g, j] = 1.0 / cntL
            pf[3, g, j] = 1.0 / cntR
            pf[1, g, j] = 1.0 / cntR if is_sample else 1.0 / w
            pf[2, g, j] = 1.0 / cntL if is_sample else 1.0 / w
    pfix = np.broadcast_to(pf.reshape(1, 128), (128, 128)).copy()
    mid = np.full((128, 1), 0.0 if is_sample else 1.0, np.float32)
    return ridx, cidx, valid, rmask.reshape(128, 1024).copy(), sm.reshape(128, 3072).copy(), pfix, mid


def _prep_common(inp):
    f = lambda a: np.ascontiguousarray(np.asarray(a, dtype=np.float32))
    rep = lambda a: np.ascontiguousarray(np.broadcast_to(np.asarray(a, np.float32)[:, None, :], (L, 128, a.shape[-1])))
    c = {}
    c["w_in"] = f(inp["w_in"])
    c["w_pool"] = f(inp["w_pool"]).reshape(L, 1024, 256)
    c["w_brp"] = f(inp["w_br_pool"]); c["w_brs"] = f(inp["w_br_sgu"]); c["w_brn"] = f(inp["w_br_na"])
    c["w_out"] = f(inp["w_out"]); c["w_up"] = f(inp["w_up"]); c["w_down"] = f(inp["w_down"])
    c["spool"] = np.ascontiguousarray(f(inp["s_pool"]).reshape(L, 8, 128).transpose(0, 2, 1))
    c["sgug"] = rep(inp["sgu_ln_g"]); c["sgub"] = rep(inp["sgu_ln_b"])
    c["wsT"] = np.ascontiguousarray(f(inp["w_s"]).transpose(0, 3, 1, 2)).reshape(L, 128, 1024)
    c["bs"] = f(inp["b_s"]).reshape(L, 1, 1024)
    c["ln1g"] = rep(inp["ln1_g"]); c["ln1b"] = rep(inp["ln1_b"])
    c["ln2g"] = rep(inp["ln2_g"]); c["ln2b"] = rep(inp["ln2_b"])
    c["ident"] = np.eye(128, dtype=np.float32)
    return c


def _prep_type(inp, is_sample):
    ridx, cidx, valid, rmask, sm, pfix, mid = _host_tables(is_sample)
    rpb = np.asarray(inp["rpb"], np.float32)
    g = rpb[:, :, ridx[:, :, None], cidx[:, None, :]]
    b2n = np.where(valid[None, None], g, np.float32(NEG)).astype(np.float32).reshape(L, 16, 128, 1024)
    return {"b2n": np.ascontiguousarray(b2n), "rm": rmask, "sm": sm, "pfix": pfix, "midflag": mid}


_NC_CACHE = {}


def kernel(**inputs):
    xp = np.asarray(inputs["x_prompt"], np.float32)
    xs = np.asarray(inputs["x_sample"], np.float32)
    common = _prep_common(inputs)
    tp = _prep_type(inputs, False)
    tsm = _prep_type(inputs, True)
    in_maps = []
    for c in range(8):
        m = dict(common)
        if c < 4:
            m.update(tp)
            m["x"] = np.ascontiguousarray(xp[c])
        else:
            m.update(tsm)
            m["x"] = np.ascontiguousarray(xs[2 * (c - 4):2 * (c - 4) + 2].reshape(NT, D))
        in_maps.append(m)
    if "nc" not in _NC_CACHE:
        _NC_CACHE["nc"] = Builder().build()
    nc = _NC_CACHE["nc"]
    res = run_bass_kernel_spmd(nc, in_maps, core_ids=list(range(8)))
    ys = [np.asarray(r["y"], np.float32) for r in res.results]
    y_prompt = np.stack(ys[0:4], axis=0)
    y_sample = np.stack(ys[4:8], axis=0).reshape(8, 4096, D)
    return (y_prompt, y_sample)
```

```python
import os
import numpy as np
from contextlib import ExitStack
import concourse.bass as bass
import concourse.mybir as mybir
from concourse.bass_utils import run_bass_kernel_spmd

F32 = mybir.dt.float32
BF16 = mybir.dt.bfloat16
AF = mybir.ActivationFunctionType
ALU = mybir.AluOpType

ENGS = ['pe', 'act', 'dve', 'pool', 'sp']
NDSEM = 32
DMA_MAXD = int(os.environ.get('K_MAXD', 512))

D = 2048
DIN = 12288
DFF = 8192
NT = 8192
L = 2
ALPHA = float((2 * L) ** 0.25)
EPS = 1e-5
NEG = -30000.0


class Op:
    __slots__ = ('fn', 'waits', 'signal', 'dma', 'epoch', 'sigval', 'didx')

    def __init__(self, fn, waits, dma, epoch, didx):
        self.fn = fn; self.waits = waits; self.signal = False; self.dma = dma
        self.epoch = epoch; self.sigval = 0; self.didx = didx


class Prog:
    def __init__(self):
        self.ops = {e: [] for e in ENGS}
        self.lastw = {}
        self.readers = {}
        self.waited = {e: {} for e in ENGS}
        self.ndma_g = 0
        self.epoch = 0
        self.last_tok = {}

    def _add(self, eng, fn, reads, writes, dma):
        deps = {}

        def need(tok):
            k = (tok[0], tok[1])
            if deps.get(k, -1) < tok[2]:
                deps[k] = tok[2]
        for k in reads:
            t = self.lastw.get(k)
            if t is not None:
                need(t)
        for k in writes:
            t = self.lastw.get(k)
            if t is not None:
                need(t)
            r = self.readers.get(k)
            if r:
                for kk, idx in r.items():
                    need((kk[0], kk[1], idx))
        waits = []
        wd = self.waited[eng]
        for k2, idx in deps.items():
            if k2[0] == 'E' and k2[1] == eng and eng == 'pe':
                continue
            if wd.get(k2, -1) >= idx:
                continue
            wd[k2] = idx
            waits.append((k2[0], k2[1], idx))
        if dma:
            didx = self.ndma_g
            self.ndma_g += 1
            slot = didx % NDSEM
            tok = ('D', slot, didx)
            if didx >= NDSEM and wd.get(('D', slot), -1) < didx - NDSEM:
                wd[('D', slot)] = didx - NDSEM
                waits.append(('D', slot, didx - NDSEM))
        else:
            didx = -1
            tok = ('E', eng, len(self.ops[eng]))
        self.ops[eng].append(Op(fn, waits, dma, self.epoch, didx))
        self.last_tok[(tok[0], tok[1])] = tok
        k2 = (tok[0], tok[1])
        for k in reads:
            r = self.readers.setdefault(k, {})
            r[k2] = tok[2]
        for k in writes:
            self.lastw[k] = tok
            self.readers[k] = {}
        return tok

    def op(self, eng, fn, reads=(), writes=()):
        return self._add(eng, fn, reads, writes, False)

    def dma(self, out, in_, reads=(), writes=(), eng='sp'):
        return self._add(eng, lambda e: e.dma_start(out=out, in_=in_), reads, writes, True)

    def barrier(self):
        toks = list(self.last_tok.values())
        for x in ENGS:
            waits = []
            for t in toks:
                if t[0] == 'E' and t[1] == x and x == 'pe':
                    continue
                k2 = (t[0], t[1])
                if self.waited[x].get(k2, -1) >= t[2]:
                    continue
                waits.append(t)
            if waits:
                self.ops[x].append(Op(None, waits, False, self.epoch, -1))
        self.epoch += 1
        self.lastw = {}
        self.readers = {}
        self.waited = {e: {} for e in ENGS}
        self.last_tok = {}

    def emit(self, nc):
        for e in ENGS:
            for op in self.ops[e]:
                for (kind, e2, idx) in op.waits:
                    if kind == 'E':
                        self.ops[e2][idx].signal = True
        for e in ENGS:
            cnt = {}
            for op in self.ops[e]:
                if op.signal:
                    cnt[op.epoch] = cnt.get(op.epoch, 0) + 1
                op.sigval = cnt.get(op.epoch, 0)
        stats = {e: (len(self.ops[e]), sum(len(o.waits) for o in self.ops[e])) for e in ENGS}
        print("PROG ops/waits:", stats, "epochs", self.epoch + 1, flush=True)
        with ExitStack() as st:
            csem = {}
            dsem = {}
            used = set()
            for e in ENGS:
                for op in self.ops[e]:
                    if op.signal:
                        used.add(('E', e, op.epoch))
            for (kind, e, ep) in sorted(used):
                csem[(e, ep)] = st.enter_context(nc.semaphore(f"s{kind}_{e}_{ep}"))
            for i in range(NDSEM):
                dsem[i] = st.enter_context(nc.semaphore(f"sD_{i}"))
            block = st.enter_context(nc.Block())
            ops = self.ops

            def run(ename, eobj):
                for op in ops[ename]:
                    for (kind, e2, idx) in op.waits:
                        if kind == 'E':
                            o2 = ops[e2][idx]
                            eobj.wait_ge(csem[(e2, o2.epoch)], o2.sigval)
                        else:
                            eobj.wait_ge(dsem[e2], 16 * (idx // NDSEM + 1))
                    if op.fn is None:
                        continue
                    ins = op.fn(eobj)
                    if op.dma:
                        ins.then_inc(dsem[op.didx % NDSEM], 16)
                    elif op.signal:
                        ins.then_inc(csem[(ename, op.epoch)], 1)

            @block.tensor
            def _(t):
                run('pe', t)

            @block.scalar
            def _(a):
                run('act', a)

            @block.vector
            def _(v):
                run('dve', v)

            @block.gpsimd
            def _(g):
                run('pool', g)

            @block.sync
            def _(s):
                run('sp', s)


ARENA_F32 = 46 * 1024


class Builder:
    def __init__(self, stop_after=None, dump=()):
        self.stop_after = stop_after
        self.dump = set(dump)
        self.nc = bass.Bass("TRN2", target_bir_lowering=False)
        self.P = Prog()
        self.hb = 0
        self.rot = 0

    def din(self, name, shape, dt=F32):
        return self.nc.dram_tensor(name, list(shape), dt, kind="ExternalInput").ap()

    def dscr(self, name, shape, dt):
        if name in self.dump:
            return self.nc.dram_tensor(name, list(shape), dt, kind="ExternalOutput").ap()
        return self.nc.dram_tensor(name, list(shape), dt).ap()

    def areset(self):
        self.aoff = 0

    def f32(self, n):
        v = self.arena[:, self.aoff:self.aoff + n]
        self.aoff += n
        assert self.aoff <= ARENA_F32, self.aoff
        return v

    def bf(self, n):
        assert n % 2 == 0
        v = self.arena[:, self.aoff:self.aoff + n // 2].bitcast(BF16)
        self.aoff += n // 2
        assert self.aoff <= ARENA_F32, self.aoff
        return v

    def nb(self):
        i = self.hb % 16
        self.hb += 1
        return self.ps[:, i * 256:(i + 1) * 256], [('bank', i // 2)]

    def nbfull(self):
        if self.hb % 2:
            self.hb += 1
        i = self.hb % 16
        self.hb += 2
        return self.ps[:, i * 256:(i + 2) * 256], [('bank', i // 2)]

    def tiles(self, n, sub=1):
        v = os.environ.get("K_TILES")
        if not v:
            return list(range(n * sub))
        base = [int(t) for t in v.split(",")]
        return [t * sub + k for t in base for k in range(sub)]

    def eng2(self):
        self.rot += 1
        return ('act', 'dve')[self.rot % 2]

    def eng3(self):
        self.rot += 1
        return ('act', 'dve', 'pool')[self.rot % 3]

    def MM(self, out, lhsT, rhs, st, sp, rd, wr):
        self.P.op('pe', lambda e: e.matmul(out, lhsT, rhs, start=st, stop=sp), rd, wr)

    def TR(self, out, in_, ident, rd, wr):
        self.P.op('pe', lambda e: e.transpose(out, in_, ident), rd, wr)

    def ACT(self, out, in_, func, rd, wr, scale=1.0, bias=None):
        if bias is None:
            self.P.op('act', lambda e: e.activation(out=out, in_=in_, func=func, scale=scale), rd, wr)
        else:
            self.P.op('act', lambda e: e.activation(out=out, in_=in_, func=func, scale=scale, bias=bias), rd, wr)

    def CP(self, eng, out, in_, rd, wr):
        if eng == 'act':
            self.P.op('act', lambda e: e.copy(out=out, in_=in_), rd, wr)
        else:
            self.P.op(eng, lambda e: e.tensor_copy(out=out, in_=in_), rd, wr)

    def TT(self, eng, out, in0, in1, op, rd, wr):
        self.P.op(eng, lambda e: e.tensor_tensor(out=out, in0=in0, in1=in1, op=op), rd, wr)

    def TS(self, eng, out, in0, s1, s2, op0, op1, rd, wr):
        if s2 is None:
            self.P.op(eng, lambda e: e.tensor_scalar(out=out, in0=in0, scalar1=s1, scalar2=None, op0=op0), rd, wr)
        else:
            self.P.op(eng, lambda e: e.tensor_scalar(out=out, in0=in0, scalar1=s1, scalar2=s2, op0=op0, op1=op1), rd, wr)

    def STT(self, eng, out, in0, scalar, in1, op0, op1, rd, wr):
        self.P.op(eng, lambda e: e.scalar_tensor_tensor(out=out, in0=in0, scalar=scalar, in1=in1, op0=op0, op1=op1), rd, wr)

    def MEMSET(self, eng, ap, val, wr):
        self.P.op(eng, lambda e: e.memset(ap, val), [], wr)

    def DMA(self, out, in_, rd=(), wr=()):
        shp = tuple(out.shape)
        if len(shp) == 3 and shp[0] * shp[1] > DMA_MAXD:
            step = max(1, DMA_MAXD // shp[0])
            for a in range(0, shp[1], step):
                b = min(shp[1], a + step)
                self.P.dma(out[:, a:b, :], in_[:, a:b, :], rd, wr)
            return
        self.P.dma(out, in_, rd, wr)

    def LN(self, xv, F, g, b, out, xkey, okey, gkeys):
        n = self.lncnt = getattr(self, 'lncnt', 0) + 1
        s = n % 2
        st = self.ln_st[s]; mv = self.ln_mv[s]
        nch = F // 512
        for k in range(nch):
            self.P.op('dve', lambda e, k=k: e.bn_stats(out=st[:, k * 6:(k + 1) * 6], in_=xv[:, k * 512:(k + 1) * 512]),
                      [xkey], [('lnst', s, k)])
        self.P.op('dve', lambda e: e.bn_aggr(out=mv[:, 0:2], in_=st[:, 0:nch * 6].rearrange("p (a b) -> p a b", b=6)),
                  [('lnst', s, k) for k in range(nch)], [('lnmv', s)])
        self.ACT(mv[:, 2:3], mv[:, 1:2], AF.Sqrt, [('lnmv', s), 'eps'], [('lnrs', s)], bias=self.eps_t[:, 0:1])
        self.P.op('dve', lambda e: e.reciprocal(out=mv[:, 3:4], in_=mv[:, 2:3]), [('lnrs', s)], [('lnrr', s)])
        self.TS('dve', xv, xv, mv[:, 0:1], mv[:, 3:4], ALU.subtract, ALU.mult, [xkey, ('lnmv', s), ('lnrr', s)], [xkey])
        self.TT('pool', xv, xv, g, ALU.mult, [xkey] + gkeys, [xkey])
        self.TT('pool', out, xv, b, ALU.add, [xkey] + gkeys, [okey] if okey != xkey else [xkey])

    def ln_setup(self):
        self.ln_st = [self.f32(24), self.f32(24)]
        self.ln_mv = [self.f32(4), self.f32(4)]
        self.eps_t = self.f32(2)
        self.MEMSET('pool', self.eps_t, EPS, ['eps'])

    def gb_setup(self):
        self.gbuf = [self.f32(512) for _ in range(2)]
        self.bbuf = [self.f32(512) for _ in range(2)]
        self.gbcnt = 0

    def LNc(self, xv, F, gsrc, bsrc, xkey):
        n = self.lncnt = getattr(self, 'lncnt', 0) + 1
        s = n % 2
        st = self.ln_st[s]; mv = self.ln_mv[s]
        nch = F // 512
        for k in range(nch):
            self.P.op('dve', lambda e, k=k: e.bn_stats(out=st[:, k * 6:(k + 1) * 6], in_=xv[:, k * 512:(k + 1) * 512]),
                      [xkey], [('lnst', s, k)])
        self.P.op('dve', lambda e: e.bn_aggr(out=mv[:, 0:2], in_=st[:, 0:nch * 6].rearrange("p (a b) -> p a b", b=6)),
                  [('lnst', s, k) for k in range(nch)], [('lnmv', s)])
        self.ACT(mv[:, 2:3], mv[:, 1:2], AF.Sqrt, [('lnmv', s), 'eps'], [('lnrs', s)], bias=self.eps_t[:, 0:1])
        self.P.op('dve', lambda e: e.reciprocal(out=mv[:, 3:4], in_=mv[:, 2:3]), [('lnrs', s)], [('lnrr', s)])
        self.TS('dve', xv, xv, mv[:, 0:1], mv[:, 3:4], ALU.subtract, ALU.mult, [xkey, ('lnmv', s), ('lnrr', s)], [xkey])
        for k in range(nch):
            self.gbcnt += 1
            m = self.gbcnt % 2
            self.DMA(self.gbuf[m], gsrc[:, k * 512:(k + 1) * 512], [], [('gbuf', m)])
            self.DMA(self.bbuf[m], bsrc[:, k * 512:(k + 1) * 512], [], [('bbuf', m)])
            xc = xv[:, k * 512:(k + 1) * 512]
            self.TT('dve', xc, xc, self.gbuf[m], ALU.mult, [xkey, ('gbuf', m)], [xkey])
            self.TT('pool', xc, xc, self.bbuf[m], ALU.add, [xkey, ('bbuf', m)], [xkey])

    def wstream_init(self, nslots, pieces, pre):
        self.w_slots = [self.bf(8192) for _ in range(nslots)]
        self.w_pieces = pieces
        self.w_issued = 0
        self.w_pre = pre

    def wget(self, n):
        ns = len(self.w_slots)
        while self.w_issued < len(self.w_pieces) and self.w_issued <= n + self.w_pre:
            m = self.w_issued
            src, kc, cols = self.w_pieces[m]
            view = self.w_slots[m % ns][:, 0:kc * cols].rearrange("p (k c) -> p k c", c=cols)
            self.DMA(view, src, [], [('w', m % ns)])
            self.w_issued += 1
        src, kc, cols = self.w_pieces[n]
        view = self.w_slots[n % ns][:, 0:kc * cols].rearrange("p (k c) -> p k c", c=cols)
        return view, ('w', n % ns)

    def build(self):
        nc = self.nc
        x = self.din("x", [NT, D])
        w_in = self.din("w_in", [L, D, DIN])
        w_pool = self.din("w_pool", [L, 1024, 256])
        w_brp = self.din("w_brp", [L, 1024, D])
        w_brs = self.din("w_brs", [L, 1024, D])
        w_brn = self.din("w_brn", [L, 1024, D])
        w_out = self.din("w_out", [L, D, D])
        w_up = self.din("w_up", [L, D, DFF])
        w_down = self.din("w_down", [L, DFF, D])
        spool = self.din("spool", [L, 128, 8])
        sgug = self.din("sgug", [L, 128, 1024])
        sgub = self.din("sgub", [L, 128, 1024])
        wsT = self.din("wsT", [L, 128, 1024])
        bs = self.din("bs", [L, 1, 1024])
        b2n = self.din("b2n", [L, 16, 128, 1024])
        rm = self.din("rm", [128, 1024])
        sm = self.din("sm", [128, 2 * 1536])
        midflag = self.din("midflag", [128, 1])
        pfix = self.din("pfix", [128, 128])
        ln1g = self.din("ln1g", [L, 128, D])
        ln1b = self.din("ln1b", [L, 128, D])
        ln2g = self.din("ln2g", [L, 128, D])
        ln2b = self.din("ln2b", [L, 128, D])
        ident = self.din("ident", [128, 128])
        y = nc.dram_tensor("y", [NT, D], F32, kind="ExternalOutput").ap()

        S = self.dscr
        wb_in = [S(f"wb_in{l}", [D, DIN], BF16) for l in range(L)]
        wb_pool = [S(f"wb_pool{l}", [1024, 256], BF16) for l in range(L)]
        wb_brp = [S(f"wb_brp{l}", [1024, D], BF16) for l in range(L)]
        wb_brs = [S(f"wb_brs{l}", [1024, D], BF16) for l in range(L)]
        wb_brn = [S(f"wb_brn{l}", [1024, D], BF16) for l in range(L)]
        wb_out = [S(f"wb_out{l}", [D, D], BF16) for l in range(L)]
        wb_up = [S(f"wb_up{l}", [D, DFF], BF16) for l in range(L)]
        wb_down = [S(f"wb_down{l}", [DFF, D], BF16) for l in range(L)]
        xT_d = S("xT_d", [D, NT], BF16)
        aT_d = S("aT_d", [1024, NT + 16], F32)
        uT_d = S("uT_d", [1024, NT], F32)
        vln_d = S("vln_d", [NT, 1024], BF16)
        qT_d = S("qT_d", [1024, NT], BF16)
        kT_d = S("kT_d", [1024, NT], BF16)
        V_d = S("V_d", [NT, 2048], BF16)
        OT_d = S("OT_d", [1024, NT], BF16)
        ypaT_d = S("ypaT_d", [1024, NT], BF16)
        ypbT_d = S("ypbT_d", [1024, NT], BF16)
        x1_d = S("x1_d", [NT, D], F32)
        x1T_d = S("x1T_d", [D, NT], BF16)
        xmid_d = S("xmid_d", [NT, D], F32)
        En_d = S("En_d", [16, 128, 1024], BF16)
        mrg_d = S("mrg_d", [D, NT], BF16)

        P = self.P
        with nc.sbuf_tensor("arena", [128, ARENA_F32], F32) as arena, \
                nc.psum_tensor("ps", [128, 4096], F32) as ps:
            self.arena = arena
            self.ps = ps

            def done(tag):
                return self.stop_after == tag

            self.areset()
            cf = [self.f32(2048) for _ in range(4)]
            cb = [self.bf(2048) for _ in range(4)]
            zt = self.f32(8)
            pieces = []
            for l in range(L):
                for (src, dst, R, C) in ((w_in[l], wb_in[l], D, DIN), (w_pool[l], wb_pool[l], 1024, 256),
                                         (w_brp[l], wb_brp[l], 1024, D), (w_brs[l], wb_brs[l], 1024, D),
                                         (w_brn[l], wb_brn[l], 1024, D), (w_out[l], wb_out[l], D, D),
                                         (w_up[l], wb_up[l], D, DFF), (w_down[l], wb_down[l], DFF, D)):
                    if self.stop_after is not None and l == 1 and not os.environ.get('K_FULLP0'):
                        continue
                    for rb in range(R // 128):
                        for c0 in range(0, C, 2048):
                            w = min(2048, C - c0)
                            pieces.append((src[rb * 128:(rb + 1) * 128, c0:c0 + w],
                                           dst[rb * 128:(rb + 1) * 128, c0:c0 + w], w))
            LAG = 2
            for n in range(len(pieces) + LAG):
                if n < len(pieces):
                    s = n % 4
                    self.DMA(cf[s][:, 0:pieces[n][2]], pieces[n][0], [], [('cf', s)])
                m = n - LAG
                if m >= 0:
                    s = m % 4
                    w = pieces[m][2]
                    self.CP(self.eng3(), cb[s][:, 0:w], cf[s][:, 0:w], [('cf', s)], [('cb', s)])
                    self.DMA(pieces[m][1], cb[s][:, 0:w], [('cb', s)], [])
            self.MEMSET('pool', zt, 0.0, ['zt'])
            for c in range(8):
                self.DMA(aT_d[c * 128:(c + 1) * 128, 0:8], zt, ['zt'], [])
                self.DMA(aT_d[c * 128:(c + 1) * 128, NT + 8:NT + 16], zt, ['zt'], [])
            P.barrier()

            for l in range(L if not done(('P0', 0)) else 0):
                xsrc = x if l == 0 else xmid_d
                xdst = xmid_d if l == 0 else y
                T = 512
                self.areset()
                self.ln_setup()
                idf = self.f32(128); idb = self.bf(128)
                xt = [self.f32(2048) for _ in range(2)]
                xtb = [self.bf(2048) for _ in range(4)]
                xT = [self.bf(16 * T).rearrange("p (k t) -> p k t", t=T) for _ in range(2)]
                st32 = [self.f32(512) for _ in range(4)]
                stb = [self.bf(512) for _ in range(4)]
                vg = self.f32(4 * 1024).rearrange("p (s c) -> p s c", c=1024)
                vlnb = [self.bf(1024) for _ in range(2)]
                Vst = [self.bf(2048) for _ in range(4)]
                gt = self.f32(1024); bt = self.f32(1024)
                self.DMA(idf, ident, [], ['idf'])
                self.CP('dve', idb, idf, ['idf'], ['idb'])
                self.DMA(gt, sgug[l], [], ['sg_g'])
                self.DMA(bt, sgub[l], [], ['sg_g'])
                for ts in range(4):
                    v3 = Vst[ts].rearrange("p (h c) -> p h c", c=128)
                    self.MEMSET('pool', v3[:, :, 64:128], 1.0, [('Vst1', ts)])
                wsrc = wb_in[l].rearrange("(k p) c -> p k c", p=128)
                tl = self.tiles(NT // T)
                jset = [int(v) for v in os.environ.get("K_JSET", "0,1,2,3,4,5,6,7,8,9,10,11").split(",")]
                pieces = []
                for i in tl:
                    for j in range(12):
                        pieces.append((wsrc[:, :, j * 512:(j + 1) * 512], 16, 512))
                self.wstream_init(3, pieces, 2)
                aT_v = aT_d.rearrange("(c p) t -> p c t", p=128)
                pn = 0
                cnt = 0
                for ii, i in enumerate(tl):
                    t0 = i * T
                    sl = ii % 2
                    for ts in range(4):
                        self.DMA(xt[ts % 2], xsrc[t0 + ts * 128:t0 + (ts + 1) * 128, :], [], [('xt', ts % 2)])
                        self.CP(('dve', 'pool')[ts % 2], xtb[ts], xt[ts % 2], [('xt', ts % 2)], [('xtb', ts)])
                    for kc in range(16):
                        bk, bkeys = self.nb()
                        pb = bk.bitcast(BF16)
                        for ts in range(4):
                            self.TR(pb[:, ts * 128:(ts + 1) * 128], xtb[ts][:, kc * 128:(kc + 1) * 128], idb,
                                    [('xtb', ts), 'idb'], bkeys)
                        self.CP(self.eng2(), xT[sl][:, kc, :], pb[:, 0:512], bkeys, [('xT', sl, kc)])
                    xkeys = [('xT', sl, kc) for kc in range(16)]
                    self.DMA(xT_d.rearrange("(k p) t -> p k t", p=128)[:, :, t0:t0 + T], xT[sl], xkeys, [])
                    for j in range(12):
                        wv, wkey = self.wget(pn); pn += 1
                        if j not in jset:
                            continue
                        if j in (0, 1, 2, 3, 6, 7, 8, 9):
                            for cc in range(4):
                                bk, bkeys = self.nbfull()
                                for kc in range(16):
                                    self.MM(bk, wv[:, kc, cc * 128:(cc + 1) * 128], xT[sl][:, kc, :], kc == 0, kc == 15,
                                            [wkey, ('xT', sl, kc)], bkeys)
                                cnt += 1
                                k = cnt % 4
                                if j < 2:
                                    ch = j * 4 + cc
                                    self.CP(self.eng2(), st32[k], bk, bkeys, [('st32', k)])
                                    self.DMA(aT_d[ch * 128:(ch + 1) * 128, 8 + t0:8 + t0 + T], st32[k], [('st32', k)], [])
                                elif j < 4:
                                    ch = (j - 2) * 4 + cc
                                    self.ACT(st32[k], bk, AF.Gelu_apprx_tanh, bkeys, [('st32', k)])
                                    self.DMA(uT_d[ch * 128:(ch + 1) * 128, t0:t0 + T], st32[k], [('st32', k)], [])
                                else:
                                    dst = qT_d if j < 8 else kT_d
                                    ch = ((j - 6) % 2) * 4 + cc
                                    self.CP(self.eng2(), stb[k], bk, bkeys, [('stb', k)])
                                    self.DMA(dst[ch * 128:(ch + 1) * 128, t0:t0 + T], stb[k], [('stb', k)], [])
                        elif j in (4, 5):
                            for ts in range(4):
                                bk, bkeys = self.nbfull()
                                for kc in range(16):
                                    self.MM(bk, xT[sl][:, kc, ts * 128:(ts + 1) * 128], wv[:, kc, :], kc == 0, kc == 15,
                                            [wkey, ('xT', sl, kc)], bkeys)
                                self.ACT(vg[:, ts, (j - 4) * 512:(j - 3) * 512], bk, AF.Gelu_apprx_tanh, bkeys,
                                         [('vg', ts)])
                            if j == 5:
                                for ts in range(4):
                                    o = vlnb[ts % 2]
                                    self.LN(vg[:, ts, :], 1024, gt, bt, o, ('vg', ts), ('vlnb', ts % 2), ['sg_g'])
                                    self.DMA(vln_d[t0 + ts * 128:t0 + (ts + 1) * 128, :], o, [('vlnb', ts % 2)], [])
                        else:
                            for ts in range(4):
                                bk, bkeys = self.nbfull()
                                for kc in range(16):
                                    self.MM(bk, xT[sl][:, kc, ts * 128:(ts + 1) * 128], wv[:, kc, :], kc == 0, kc == 15,
                                            [wkey, ('xT', sl, kc)], bkeys)
                                v3 = Vst[ts].rearrange("p (h c) -> p h c", c=128)
                                self.CP(self.eng2(), v3[:, (j - 10) * 8:(j - 9) * 8, 0:64],
                                        bk.rearrange("p (h c) -> p h c", c=64), bkeys, [('Vst', ts, j - 10)])
                            if j == 11:
                                for ts in range(4):
                                    self.DMA(V_d[t0 + ts * 128:t0 + (ts + 1) * 128, :], Vst[ts],
                                             [('Vst', ts, 0), ('Vst', ts, 1), ('Vst1', ts)], [('Vst', ts, 0), ('Vst', ts, 1)])
                P.barrier()
                if done(('P1', l)):
                    break

                self.areset()
                Em = self.bf(16 * 1024).rearrange("p (h c) -> p h c", c=1024)
                smb = self.bf(2 * 1536)
                mark = self.aoff
                btmp = [self.f32(1024) for _ in range(2)]
                en32 = [self.f32(1024) for _ in range(2)]
                rmt = self.f32(1024)
                enb0 = [self.bf(1024) for _ in range(2)]
                smf = self.f32(2 * 1536)
                self.DMA(rmt, rm, [], ['rm'])
                self.DMA(smf, sm, [], ['smf'])
                self.CP('dve', smb, smf, ['smf'], ['smb'])
                for h in range(16):
                    s = h % 2
                    self.DMA(btmp[s], b2n[l, h], [], [('btmp', s)])
                    self.ACT(en32[s], btmp[s], AF.Exp, [('btmp', s)], [('en32', s)])
                    self.TT('pool', Em[:, h, :], en32[s], rmt, ALU.mult, [('en32', s), 'rm'], [('Em', h)])
                    self.CP('dve', enb0[s], en32[s], [('en32', s)], [('enb0', s)])
                    self.DMA(En_d[h], enb0[s], [('enb0', s)], [])
                P.barrier()
                self.aoff = mark
                T = 512
                qTt = [self.bf(8 * T).rearrange("p (k t) -> p k t", t=T) for _ in range(2)]
                kTw = [self.bf(8 * 1024).rearrange("p (k t) -> p k t", t=1024) for _ in range(2)]
                Vr = [self.bf(2048) for _ in range(12)]
                OTt = [self.bf(8 * T).rearrange("p (k t) -> p k t", t=T) for _ in range(2)]
                eS = [self.f32(1536) for _ in range(2)]
                Pm = [self.bf(1536) for _ in range(2)]
                enb = [self.bf(1024) for _ in range(2)]
                rd = [self.f32(256) for _ in range(2)]
                qT_v = qT_d.rearrange("(k p) t -> p k t", p=128)
                kT_v = kT_d.rearrange("(k p) t -> p k t", p=128)
                OT_v = OT_d.rearrange("(k p) t -> p k t", p=128)
                tl = self.tiles(NT // T)
                consec = (tl == list(range(NT // T)))

                def na_loads(ii):
                    i = tl[ii]
                    sl = ii % 2
                    t0 = i * T
                    self.DMA(qTt[sl], qT_v[:, :, t0:t0 + T], [], [('qTt', sl)])
                    lo = max(0, 4 * i - 2); hi = min(63, 4 * i + 5)
                    o0 = (lo - (4 * i - 2)) * 128
                    self.DMA(kTw[sl][:, :, o0:o0 + (hi - lo + 1) * 128], kT_v[:, :, lo * 128:(hi + 1) * 128], [], [('kTw', sl)])
                    if consec:
                        cs = range(0, 6) if i == 0 else range(4 * i + 2, min(63, 4 * i + 5) + 1)
                    else:
                        cs = range(lo, hi + 1)
                    for c in cs:
                        self.DMA(Vr[c % 12], V_d[c * 128:(c + 1) * 128, :], [], [('Vr', c % 12)])

                na_loads(0)
                encnt = [0]
                units = [(ii, i, b, h) for ii, i in enumerate(tl) for b in range(2) for h in range(16)]

                def na_geom(i, b):
                    B = 2 * i + b
                    sval = [s for s in range(6) if 0 <= 2 * B + (3 - s) <= 63]
                    return B, sval, sval[0], sval[-1] + 1

                def na_stage_a(u, n):
                    ii, i, b, h = u
                    sl = ii % 2
                    B, sval, s0, s1 = na_geom(i, b)
                    ns = s1 - s0
                    po = (h % 2) * 64
                    hp = h // 2
                    psS = self.ps[:, n * 1536:(n + 1) * 1536]
                    for s in sval:
                        c = 2 * B + 3 - s
                        sk = c - (4 * i - 2)
                        self.MM(psS[:, s * 256:(s + 1) * 256], kTw[sl][po:po + 64, hp, sk * 128:(sk + 1) * 128],
                                qTt[sl][po:po + 64, hp, b * 256:(b + 1) * 256], True, True,
                                [('kTw', sl), ('qTt', sl)], [('psS', n)])
                    self.ACT(eS[n][:, s0 * 256:s1 * 256], psS[:, s0 * 256:s1 * 256], AF.Exp, [('psS', n)], [('eS', n)],
                             scale=0.125)
                    if B in (0, 15, 16, 31):
                        m = encnt[0] % 2
                        encnt[0] += 1
                        self.DMA(enb[m], En_d[h], [], [('enb', m)])
                        tab = enb[m]
                        tkey = ('enb', m)
                    else:
                        tab = Em[:, h, :]
                        tkey = ('Em', h)
                    tap = bass.AP(tab.tensor, tab.offset + 64 + 128 * s0, [list(tab.ap[0]), [128, ns], [1, 256]])
                    pm3 = Pm[n][:, s0 * 256:s1 * 256].rearrange("p (s q) -> p s q", q=256)
                    es3 = eS[n][:, s0 * 256:s1 * 256].rearrange("p (s q) -> p s q", q=256)
                    self.TT('pool', pm3, es3, tap, ALU.mult, [('eS', n), tkey], [('Pm', n)])
                    if B in (15, 16):
                        o = (B - 15) * 1536
                        self.TT('pool', Pm[n][:, s0 * 256:s1 * 256], Pm[n][:, s0 * 256:s1 * 256],
                                smb[:, o + s0 * 256:o + s1 * 256], ALU.mult, [('Pm', n), 'smb'], [('Pm', n)])

                def na_stage_b(u, n):
                    ii, i, b, h = u
                    sl = ii % 2
                    B, sval, s0, s1 = na_geom(i, b)
                    po = (h % 2) * 64
                    hp = h // 2
                    psO = self.ps[:, 3072 + n * 512:3072 + n * 512 + 256]
                    for s in sval:
                        c = 2 * B + 3 - s
                        self.MM(psO, Vr[c % 12][:, h * 128:(h + 1) * 128], Pm[n][:, s * 256:(s + 1) * 256],
                                s == s0, s == s1 - 1, [('Vr', c % 12), ('Pm', n)], [('psO', n)])
                    self.P.op('dve', lambda e, psO=psO, r=rd[n]: e.reciprocal(out=r[0:64, :], in_=psO[64:128, :]),
                              [('psO', n)], [('rd', n)])
                    self.TT('dve', OTt[sl][po:po + 64, hp, b * 256:(b + 1) * 256], psO[0:64, :], rd[n][0:64, :],
                            ALU.mult, [('psO', n), ('rd', n)], [('OTt', sl)])

                nu = len(units)
                for k in range(nu + 1):
                    if k < nu:
                        na_stage_a(units[k], k % 2)
                    if k >= 1:
                        u = units[k - 1]
                        na_stage_b(u, (k - 1) % 2)
                        if k == nu or units[k][0] != u[0]:
                            t0 = u[1] * T
                            self.DMA(OT_v[:, :, t0:t0 + T], OTt[u[0] % 2], [('OTt', u[0] % 2)], [])
                    if k < nu and (k == 0 or units[k - 1][0] != units[k][0]):
                        ii = units[k][0]
                        if ii + 1 < len(tl):
                            na_loads(ii + 1)
                P.barrier()
                if done(('P2a', l)):
                    break

                self.areset()
                T = 512
                A_ = [self.f32(8 * 528).rearrange("p (c t) -> p c t", t=528) for _ in range(2)]
                T1 = self.f32(2 * 528).rearrange("p (c t) -> p c t", t=528)
                T2 = self.f32(2 * 528).rearrange("p (c t) -> p c t", t=528)
                T3 = self.f32(2 * 528).rearrange("p (c t) -> p c t", t=528)
                U1 = self.f32(2 * 528).rearrange("p (c t) -> p c t", t=528)
                U2 = self.f32(2 * 528).rearrange("p (c t) -> p c t", t=528)
                U3 = self.f32(2 * 528).rearrange("p (c t) -> p c t", t=528)
                tmp8 = self.f32(16).rearrange("p (c t) -> p c t", t=8)
                pT = self.bf(8 * T).rearrange("p (c t) -> p c t", t=T)
                ypa = [self.bf(8 * T).rearrange("p (c t) -> p c t", t=T) for _ in range(2)]
                uT = [self.f32(8 * T).rearrange("p (c t) -> p c t", t=T) for _ in range(2)]
                vln = [self.bf(4 * 1024).rearrange("p (s c) -> p s c", c=1024) for _ in range(2)]
                ypb = [self.bf(8 * T).rearrange("p (c t) -> p c t", t=T) for _ in range(2)]
                wpl = self.bf(8 * 256).rearrange("p (g k c) -> p g k c", g=4, k=2)
                spl = self.f32(8)
                wsf = self.f32(1024); wsb = self.bf(1024).rearrange("p (g q) -> p g q", q=128)
                bsf = self.f32(1024); ones = self.f32(128)
                mfl = self.f32(2); pfx = self.f32(128)
                self.DMA(wpl, wb_pool[l].rearrange("(g k p) c -> p g k c", g=4, k=2), [], ['wpl'])
                self.DMA(spl, spool[l], [], ['spl'])
                self.DMA(wsf, wsT[l], [], ['wsf'])
                self.CP('dve', wsb.rearrange("p g q -> p (g q)"), wsf, ['wsf'], ['wsb'])
                self.DMA(bsf[0:1, :], bs[l], [], ['bsf'])
                self.MEMSET('pool', ones, 1.0, ['ones'])
                self.DMA(mfl[:, 0:1], midflag, [], ['mfl'])
                self.DMA(pfx, pfix, [], ['pfx'])
                aT_v = aT_d.rearrange("(c p) t -> p c t", p=128)
                uT_v = uT_d.rearrange("(c p) t -> p c t", p=128)
                vln_v = vln_d.rearrange("(n p) c -> p n c", p=128)
                ypa_v = ypaT_d.rearrange("(c p) t -> p c t", p=128)
                ypb_v = ypbT_d.rearrange("(c p) t -> p c t", p=128)
                tl = self.tiles(NT // T)

                def pb_loads(ii):
                    i = tl[ii]
                    sl = ii % 2
                    t0 = i * T
                    self.DMA(A_[sl], aT_v[:, :, t0:t0 + 528], [], [('A', sl, g) for g in range(4)])
                    self.DMA(uT[sl], uT_v[:, :, t0:t0 + T], [], [('uT', sl)])
                    self.DMA(vln[sl], vln_v[:, 4 * i:4 * i + 4, :], [], [('vln', sl)])
                pb_loads(0)
                for ii, i in enumerate(tl):
                    sl = ii % 2
                    t0 = i * T
                    if ii + 1 < len(tl):
                        pb_loads(ii + 1)
                    Akeys = [('A', sl, g) for g in range(4)]
                    if i == 7:
                        self.TS('pool', A_[sl][:, :, 520:528], A_[sl][:, :, 520:528], mfl[:, 0:1], None, ALU.mult, None,
                                Akeys + ['mfl'], Akeys)
                    if i == 8:
                        self.TS('pool', A_[sl][:, :, 0:8], A_[sl][:, :, 0:8], mfl[:, 0:1], None, ALU.mult, None,
                                Akeys + ['mfl'], Akeys)
                    bidx = {0: (0, 0), 7: (1, 504), 8: (2, 0), 15: (3, 504)}.get(i)
                    for g in range(4):
                        eng = ('pool', 'dve')[g % 2]
                        X1, X2, X3 = (T1, T2, T3) if g % 2 == 0 else (U1, U2, U3)
                        kx = 'T' if g % 2 == 0 else 'U'
                        Av = A_[sl][:, 2 * g:2 * g + 2, :]
                        ak = [('A', sl, g)]
                        if g == 0:
                            self.TT(eng, X1[:, :, 0:512], Av[:, :, 7:519], Av[:, :, 8:520], ALU.add, ak, [(kx, 1)])
                            Wv = X1; wk = (kx, 1)
                        elif g == 1:
                            self.TT(eng, X1[:, :, 0:527], Av[:, :, 0:527], Av[:, :, 1:528], ALU.add, ak, [(kx, 1)])
                            self.TT(eng, X2[:, :, 0:512], X1[:, :, 6:518], X1[:, :, 8:520], ALU.add, [(kx, 1)], [(kx, 2)])
                            Wv = X2; wk = (kx, 2)
                        elif g == 2:
                            self.TT(eng, X1[:, :, 0:527], Av[:, :, 0:527], Av[:, :, 1:528], ALU.add, ak, [(kx, 1)])
                            self.TT(eng, X2[:, :, 0:525], X1[:, :, 0:525], X1[:, :, 2:527], ALU.add, [(kx, 1)], [(kx, 2)])
                            self.TT(eng, X3[:, :, 0:512], X2[:, :, 4:516], X2[:, :, 8:520], ALU.add, [(kx, 2)], [(kx, 3)])
                            Wv = X3; wk = (kx, 3)
                        else:
                            self.TT(eng, X1[:, :, 0:527], Av[:, :, 0:527], Av[:, :, 1:528], ALU.add, ak, [(kx, 1)])
                            self.TT(eng, X2[:, :, 0:525], X1[:, :, 0:525], X1[:, :, 2:527], ALU.add, [(kx, 1)], [(kx, 2)])
                            self.TT(eng, X3[:, :, 0:521], X2[:, :, 0:521], X2[:, :, 4:525], ALU.add, [(kx, 2)], [(kx, 3)])
                            self.TT(eng, X1[:, :, 0:512], X3[:, :, 0:512], X3[:, :, 8:520], ALU.add, [(kx, 3)], [(kx, 1)])
                            Wv = X1; wk = (kx, 1)
                        wdt = 2 ** (g + 1)
                        self.STT('dve', pT[:, 2 * g:2 * g + 2, :], Wv[:, :, 0:512], 1.0 / wdt, Av[:, :, 8:520],
                                 ALU.mult, ALU.subtract, [wk] + ak, [('pT', g)])
                        if bidx is not None:
                            bi, c0 = bidx
                            pf = bass.AP(pfx.tensor, pfx.offset + bi * 32 + g * 8, [list(pfx.ap[0]), [0, 2], [1, 8]])
                            self.TT(eng, tmp8, Wv[:, :, c0:c0 + 8], pf, ALU.mult, [wk, 'pfx'], ['tmp8'])
                            self.TT(eng, pT[:, 2 * g:2 * g + 2, c0:c0 + 8], tmp8, Av[:, :, 8 + c0:16 + c0], ALU.subtract,
                                    ['tmp8', ('pT', g)] + ak, [('pT', g)])
                    for g in range(4):
                        for oc in range(2):
                            bk, bkeys = self.nbfull()
                            for kc in range(2):
                                self.MM(bk, wpl[:, g, kc, oc * 128:(oc + 1) * 128], pT[:, 2 * g + kc, :], kc == 0, kc == 1,
                                        ['wpl', ('pT', g)], bkeys)
                            self.ACT(ypa[sl][:, 2 * g + oc, :], bk, AF.Copy, bkeys + ['spl'], [('ypa', sl)],
                                     scale=spl[:, 2 * g + oc:2 * g + oc + 1])
                    self.DMA(ypa_v[:, :, t0:t0 + T], ypa[sl], [('ypa', sl)], [])
                    for g in range(8):
                        bk, bkeys = self.nbfull()
                        for ts in range(4):
                            self.MM(bk[:, ts * 128:(ts + 1) * 128], vln[sl][:, ts, g * 128:(g + 1) * 128], wsb[:, g, :],
                                    True, False, [('vln', sl), 'wsb'], bkeys)
                            self.MM(bk[:, ts * 128:(ts + 1) * 128], ones[0:1, 0:128], bsf[0:1, g * 128:(g + 1) * 128],
                                    False, True, ['ones', 'bsf'], bkeys)
                        self.TT('dve', ypb[sl][:, g, :], bk, uT[sl][:, g, :], ALU.mult, bkeys + [('uT', sl)], [('ypb', sl)])
                    self.DMA(ypb_v[:, :, t0:t0 + T], ypb[sl], [('ypb', sl)], [])
                P.barrier()
                if done(('P2b', l)):
                    break

                self.areset()
                T = 512
                tl = self.tiles(NT // T)
                xTt = [self.bf(16 * T).rearrange("p (k t) -> p k t", t=T) for _ in range(2)]
                ypat = self.bf(8 * T).rearrange("p (k t) -> p k t", t=T)
                ypbt = self.bf(8 * T).rearrange("p (k t) -> p k t", t=T)
                OTt3 = self.bf(8 * T).rearrange("p (k t) -> p k t", t=T)
                sg = [self.f32(4 * T).rearrange("p (k t) -> p k t", t=T) for _ in range(2)]
                tt = [self.f32(T) for _ in range(2)]
                acc = self.f32(4 * T).rearrange("p (k t) -> p k t", t=T)
                mrg = [self.bf(16 * T).rearrange("p (k t) -> p k t", t=T) for _ in range(2)]
                win_v = wb_in[l].rearrange("(k p) c -> p k c", p=128)
                brv = [w.rearrange("(k p) c -> p k c", p=128) for w in (wb_brp[l], wb_brs[l], wb_brn[l])]
                pieces = []
                for i in tl:
                    for dcg in range(4):
                        for br in range(3):
                            c0 = 6144 + br * 2048 + dcg * 512
                            pieces.append((win_v[:, :, c0:c0 + 512], 16, 512))
                            pieces.append((brv[br][:, :, dcg * 512:(dcg + 1) * 512], 8, 512))
                self.wstream_init(4, pieces, 3)
                xT_v = xT_d.rearrange("(k p) t -> p k t", p=128)
                OT_v = OT_d.rearrange("(k p) t -> p k t", p=128)
                mrg_v = mrg_d.rearrange("(k p) t -> p k t", p=128)
                brin = (ypat, ypbt, OTt3)
                brsrc = (ypa_v, ypb_v, OT_v)

                def p3_loads(ii):
                    i = tl[ii]
                    t0 = i * T
                    self.DMA(xTt[ii % 2], xT_v[:, :, t0:t0 + T], [], [('xTt', ii % 2)])
                    for br in range(3):
                        self.DMA(brin[br], brsrc[br][:, :, t0:t0 + T], [], [('brin', br)])
                p3_loads(0)
                pn = 0
                c2 = 0
                for ii, i in enumerate(tl):
                    sl = ii % 2
                    t0 = i * T
                    for dcg in range(4):
                        for br in range(3):
                            wv, wkey = self.wget(pn); pn += 1
                            sgi = (dcg * 3 + br) % 2
                            for dc4 in range(4):
                                bk, bkeys = self.nbfull()
                                for kc in range(16):
                                    self.MM(bk, wv[:, kc, dc4 * 128:(dc4 + 1) * 128], xTt[sl][:, kc, :], kc == 0, kc == 15,
                                            [wkey, ('xTt', sl)], bkeys)
                                self.ACT(sg[sgi][:, dc4, :], bk, AF.Sigmoid, bkeys, [('sg', sgi, dc4)])
                            wv, wkey = self.wget(pn); pn += 1
                            for dc4 in range(4):
                                bk, bkeys = self.nbfull()
                                for kc in range(8):
                                    self.MM(bk, wv[:, kc, dc4 * 128:(dc4 + 1) * 128], brin[br][:, kc, :], kc == 0, kc == 7,
                                            [wkey, ('brin', br)], bkeys)
                                if br == 0:
                                    self.TT('dve', acc[:, dc4, :], bk, sg[sgi][:, dc4, :], ALU.mult,
                                            bkeys + [('sg', sgi, dc4)], [('acc', dc4)])
                                else:
                                    c2 += 1
                                    m = c2 % 2
                                    self.TT('dve', tt[m], bk, sg[sgi][:, dc4, :], ALU.mult,
                                            bkeys + [('sg', sgi, dc4)], [('tt', m)])
                                    if br == 1:
                                        self.TT('pool', acc[:, dc4, :], acc[:, dc4, :], tt[m], ALU.add,
                                                [('acc', dc4), ('tt', m)], [('acc', dc4)])
                                    else:
                                        self.TT('pool', mrg[sl][:, dcg * 4 + dc4, :], acc[:, dc4, :], tt[m], ALU.add,
                                                [('acc', dc4), ('tt', m)], [('mrg', sl)])
                    self.DMA(mrg_v[:, :, t0:t0 + T], mrg[sl], [('mrg', sl)], [])
                    if ii + 1 < len(tl):
                        p3_loads(ii + 1)
                P.barrier()
                if done(('P3a', l)):
                    break

                self.areset()
                self.ln_setup()
                T = 512
                tl = self.tiles(NT // T)
                idf = self.f32(128); idb = self.bf(128)
                self.DMA(idf, ident, [], ['idf'])
                self.CP('dve', idb, idf, ['idf'], ['idb'])
                self.gb_setup()
                mrgt = [self.bf(16 * T).rearrange("p (k t) -> p k t", t=T) for _ in range(2)]
                xtk = self.f32(4 * D).rearrange("p (s c) -> p s c", c=D)
                mT = [self.f32(4 * T).rearrange("p (k t) -> p k t", t=T) for _ in range(2)]
                x1b = [self.bf(D) for _ in range(4)]
                x1T = self.bf(16 * T).rearrange("p (k t) -> p k t", t=T)
                wout_v = wb_out[l].rearrange("(k p) c -> p k c", p=128)
                pieces = []
                for i in tl:
                    for cg in range(4):
                        pieces.append((wout_v[:, :, cg * 512:(cg + 1) * 512], 16, 512))
                self.wstream_init(3, pieces, 2)
                x1T_v = x1T_d.rearrange("(k p) t -> p k t", p=128)
                self.DMA(mrgt[0], mrg_v[:, :, tl[0] * T:tl[0] * T + T], [], [('mrgt', 0)])
                pn = 0

                def p3b_mm(cg, sl):
                    nonlocal pn
                    wv, wkey = self.wget(pn); pn += 1
                    mi = cg % 2
                    for dc4 in range(4):
                        bk, bkeys = self.nbfull()
                        for kc in range(16):
                            self.MM(bk, wv[:, kc, dc4 * 128:(dc4 + 1) * 128], mrgt[sl][:, kc, :], kc == 0, kc == 15,
                                    [wkey, ('mrgt', sl)], bkeys)
                        self.CP('act', mT[mi][:, dc4, :], bk, bkeys, [('mT', mi, dc4)])

                def p3b_tr(cg):
                    mi = cg % 2
                    for ts in range(4):
                        bk, bkeys = self.nbfull()
                        for dc4 in range(4):
                            self.TR(bk[:, dc4 * 128:(dc4 + 1) * 128], mT[mi][:, dc4, ts * 128:(ts + 1) * 128], idf,
                                    [('mT', mi, dc4), 'idf'], bkeys)
                        xv = xtk[:, ts, cg * 512:(cg + 1) * 512]
                        self.STT('dve', xv, xv, ALPHA, bk, ALU.mult, ALU.add, bkeys + [('xtk', ts)], [('xtk', ts)])

                def p3b_tail_nonpe(t0):
                    for ts in range(4):
                        self.LNc(xtk[:, ts, :], D, ln1g[l], ln1b[l], ('xtk', ts))
                        self.DMA(x1_d[t0 + ts * 128:t0 + (ts + 1) * 128, :], xtk[:, ts, :], [('xtk', ts)], [])
                        self.CP('act', x1b[ts], xtk[:, ts, :], [('xtk', ts)], [('x1b', ts)])

                def p3b_tail_pe(t0):
                    for ts in range(4):
                        for kq in range(4):
                            bk, bkeys = self.nb()
                            pb = bk.bitcast(BF16)
                            for k4 in range(4):
                                kc = kq * 4 + k4
                                self.TR(pb[:, k4 * 128:(k4 + 1) * 128], x1b[ts][:, kc * 128:(kc + 1) * 128], idb,
                                        [('x1b', ts), 'idb'], bkeys)
                            self.CP(self.eng2(), x1T[:, kq * 4:kq * 4 + 4, ts * 128:(ts + 1) * 128],
                                    pb.rearrange("p (k t) -> p k t", t=128), bkeys, [('x1T', ts, kq)])
                    self.DMA(x1T_v[:, :, t0:t0 + T], x1T, [('x1T', ts, kq) for ts in range(4) for kq in range(4)], [])

                for ii, i in enumerate(tl):
                    sl = ii % 2
                    t0 = i * T
                    if ii + 1 < len(tl):
                        tn = tl[ii + 1] * T
                        self.DMA(mrgt[1 - sl], mrg_v[:, :, tn:tn + T], [], [('mrgt', 1 - sl)])
                    p3b_mm(0, sl)
                    p3b_mm(1, sl)
                    if ii > 0:
                        p3b_tail_pe(tl[ii - 1] * T)
                    self.DMA(xtk, xsrc.rearrange("(n p) c -> p n c", p=128)[:, 4 * i:4 * i + 4, :], [],
                             [('xtk', ts) for ts in range(4)])
                    p3b_tr(0)
                    p3b_tr(1)
                    p3b_mm(2, sl)
                    p3b_tr(2)
                    p3b_mm(3, sl)
                    p3b_tr(3)
                    p3b_tail_nonpe(t0)
                p3b_tail_pe(tl[-1] * T)
                P.barrier()
                if done(('P3', l)):
                    break

                self.areset()
                self.ln_setup()
                T = 512
                tl = self.tiles(NT // T)
                idf = self.f32(128)
                self.DMA(idf, ident, [], ['idf'])
                self.gb_setup()
                x1Tt = self.bf(16 * T).rearrange("p (k t) -> p k t", t=T)
                x1k = self.f32(4 * D).rearrange("p (s c) -> p s c", c=D)
                hT = self.bf(64 * T).rearrange("p (k t) -> p k t", t=T)
                r32 = [self.f32(T) for _ in range(2)]
                fT = self.f32(4 * T).rearrange("p (k t) -> p k t", t=T)
                wup_v = wb_up[l].rearrange("(k p) c -> p k c", p=128)
                wdn_v = wb_down[l].rearrange("(k p) c -> p k c", p=128)
                pieces = []
                for i in tl:
                    for j in range(16):
                        pieces.append((wup_v[:, :, j * 512:(j + 1) * 512], 16, 512))
                    for cg in range(4):
                        for q in range(4):
                            pieces.append((wdn_v[:, 16 * q:16 * q + 16, cg * 512:(cg + 1) * 512], 16, 512))
                self.wstream_init(3, pieces, 2)
                self.DMA(x1Tt, x1T_v[:, :, tl[0] * T:tl[0] * T + T], [], ['x1Tt'])
                pn = 0
                c4 = 0
                for ii, i in enumerate(tl):
                    t0 = i * T
                    for j in range(16):
                        wv, wkey = self.wget(pn); pn += 1
                        for fc4 in range(4):
                            bk, bkeys = self.nbfull()
                            for kc in range(16):
                                self.MM(bk, wv[:, kc, fc4 * 128:(fc4 + 1) * 128], x1Tt[:, kc, :], kc == 0, kc == 15,
                                        [wkey, 'x1Tt'], bkeys)
                            c4 += 1
                            m = c4 % 2
                            self.ACT(r32[m], bk, AF.Relu, bkeys, [('r32', m)])
                            self.TT('pool', hT[:, j * 4 + fc4, :], r32[m], r32[m], ALU.mult, [('r32', m)], [('hT', j * 4 + fc4)])
                    self.DMA(x1k, x1_d.rearrange("(n p) c -> p n c", p=128)[:, 4 * i:4 * i + 4, :], [],
                             [('x1k', ts) for ts in range(4)])
                    if ii + 1 < len(tl):
                        tn = tl[ii + 1] * T
                        self.DMA(x1Tt, x1T_v[:, :, tn:tn + T], [], ['x1Tt'])

                    def down_tr(cg):
                        for ts in range(4):
                            bi = (cg % 2) * 4 + ts
                            bk = self.ps[:, bi * 512:(bi + 1) * 512]
                            bkeys = [('bank', bi)]
                            for dc4 in range(4):
                                self.TR(bk[:, dc4 * 128:(dc4 + 1) * 128], fT[:, dc4, ts * 128:(ts + 1) * 128], idf,
                                        [('fT', dc4), 'idf'], bkeys)
                            xv = x1k[:, ts, cg * 512:(cg + 1) * 512]
                            self.STT('dve', xv, xv, ALPHA, bk, ALU.mult, ALU.add, bkeys + [('x1k', ts)], [('x1k', ts)])

                    for cg in range(4):
                        banks = []
                        for dc4 in range(4):
                            bi = (cg % 2) * 4 + dc4
                            banks.append((self.ps[:, bi * 512:(bi + 1) * 512], [('bank', bi)]))
                        for q in range(4):
                            wv, wkey = self.wget(pn); pn += 1
                            for dc4 in range(4):
                                bk, bkeys = banks[dc4]
                                for kc in range(16):
                                    self.MM(bk, wv[:, kc, dc4 * 128:(dc4 + 1) * 128], hT[:, 16 * q + kc, :],
                                            q == 0 and kc == 0, q == 3 and kc == 15, [wkey, ('hT', 16 * q + kc)], bkeys)
                        if cg > 0:
                            down_tr(cg - 1)
                        for dc4 in range(4):
                            bk, bkeys = banks[dc4]
                            self.CP('act', fT[:, dc4, :], bk, bkeys, [('fT', dc4)])
                    down_tr(3)
                    for ts in range(4):
                        self.LNc(x1k[:, ts, :], D, ln2g[l], ln2b[l], ('x1k', ts))
                        self.DMA(xdst[t0 + ts * 128:t0 + (ts + 1) * 128, :], x1k[:, ts, :], [('x1k', ts)], [])
                P.barrier()
                if done(('P4', l)):
                    break
            P.emit(nc)
        return nc


def _host_tables(is_sample):
    p = np.arange(128)
    kc = p % 64
    up = (p >= 64).astype(np.int64)
    e = np.arange(16)
    qc = np.arange(64)
    dr = (7 - e)[None, :] + up[:, None]
    c0 = np.clip(qc - 8, 0, 48)
    colok = (kc[:, None] >= c0[None, :]) & (kc[:, None] < c0[None, :] + 16)
    drok = np.abs(dr) <= 7
    ridx = np.clip(dr + 7, 0, 14)
    cidx = np.clip(kc[:, None] - qc[None, :] + 15, 0, 30)
    valid = drok[:, :, None] & colok[:, None, :]
    rmask = ((dr >= -4) & (dr <= 3)).astype(np.float32)[:, :, None] * np.ones((1, 1, 64), np.float32)
    sm = np.zeros((128, 2, 6, 4, 64), np.float32)
    for bi, B in enumerate((15, 16)):
        for s in range(6):
            cr = 3 - s
            for r in range(4):
                qr = 4 * B + r
                for u in range(2):
                    kr = 4 * B + 2 * cr + u
                    if is_sample:
                        ok = (kr // 64 == qr // 64) and (0 <= kr < 128)
                        r0 = min(max(qr % 64 - 4, 0), 56)
                        ok = ok and (r0 <= kr % 64 <= r0 + 7)
                    else:
                        r0 = min(max(qr - 4, 0), 120)
                        ok = (r0 <= kr <= r0 + 7)
                    if ok:
                        sm[u * 64:(u + 1) * 64, bi, s, r, :] = 1.0
    pf = np.zeros((4, 4, 8), np.float32)
    for g, w in enumerate((2, 4, 8, 16)):
        half = w // 2
        sl = 4096 if is_sample else 8192
        for j in range(8):
            t = j
            cntL = min(t + half, sl) - max(t - half, 0)
            t = sl - 8 + j
            cntR = min(t + half, sl) - max(t - half, 0)
            pf[0, g, j] = 1.0 / cntL
            pf[3, g, j] = 1.0 / cntR
            pf[1, g, j] = 1.0 / cntR if is_sample else 1.0 / w
            pf[2, g, j] = 1.0 / cntL if is_sample else 1.0 / w
    pfix = np.broadcast_to(pf.reshape(1, 128), (128, 128)).copy()
    mid = np.full((128, 1), 0.0 if is_sample else 1.0, np.float32)
    return ridx, cidx, valid, rmask.reshape(128, 1024).copy(), sm.reshape(128, 3072).copy(), pfix, mid


def _prep_common(inp):
    f = lambda a: np.ascontiguousarray(np.asarray(a, dtype=np.float32))
    rep = lambda a: np.ascontiguousarray(np.broadcast_to(np.asarray(a, np.float32)[:, None, :], (L, 128, a.shape[-1])))
    c = {}
    c["w_in"] = f(inp["w_in"])
    c["w_pool"] = f(inp["w_pool"]).reshape(L, 1024, 256)
    c["w_brp"] = f(inp["w_br_pool"]); c["w_brs"] = f(inp["w_br_sgu"]); c["w_brn"] = f(inp["w_br_na"])
    c["w_out"] = f(inp["w_out"]); c["w_up"] = f(inp["w_up"]); c["w_down"] = f(inp["w_down"])
    c["spool"] = np.ascontiguousarray(f(inp["s_pool"]).reshape(L, 8, 128).transpose(0, 2, 1))
    c["sgug"] = rep(inp["sgu_ln_g"]); c["sgub"] = rep(inp["sgu_ln_b"])
    c["wsT"] = np.ascontiguousarray(f(inp["w_s"]).transpose(0, 3, 1, 2)).reshape(L, 128, 1024)
    c["bs"] = f(inp["b_s"]).reshape(L, 1, 1024)
    c["ln1g"] = rep(inp["ln1_g"]); c["ln1b"] = rep(inp["ln1_b"])
    c["ln2g"] = rep(inp["ln2_g"]); c["ln2b"] = rep(inp["ln2_b"])
    c["ident"] = np.eye(128, dtype=np.float32)
    return c


def _prep_type(inp, is_sample):
    ridx, cidx, valid, rmask, sm, pfix, mid = _host_tables(is_sample)
    rpb = np.asarray(inp["rpb"], np.float32)
    g = rpb[:, :, ridx[:, :, None], cidx[:, None, :]]
    b2n = np.where(valid[None, None], g, np.float32(NEG)).astype(np.float32).reshape(L, 16, 128, 1024)
    return {"b2n": np.ascontiguousarray(b2n), "rm": rmask, "sm": sm, "pfix": pfix, "midflag": mid}


_NC_CACHE = {}


def kernel(**inputs):
    xp = np.asarray(inputs["x_prompt"], np.float32)
    xs = np.asarray(inputs["x_sample"], np.float32)
    common = _prep_common(inputs)
    tp = _prep_type(inputs, False)
    tsm = _prep_type(inputs, True)
    in_maps = []
    for c in range(8):
        m = dict(common)
        if c < 4:
            m.update(tp)
            m["x"] = np.ascontiguousarray(xp[c])
        else:
            m.update(tsm)
            m["x"] = np.ascontiguousarray(xs[2 * (c - 4):2 * (c - 4) + 2].reshape(NT, D))
        in_maps.append(m)
    if "nc" not in _NC_CACHE:
        _NC_CACHE["nc"] = Builder().build()
    nc = _NC_CACHE["nc"]
    res = run_bass_kernel_spmd(nc, in_maps, core_ids=list(range(8)))
    ys = [np.asarray(r["y"], np.float32) for r in res.results]
    y_prompt = np.stack(ys[0:4], axis=0)
    y_sample = np.stack(ys[4:8], axis=0).reshape(8, 4096, D)
    return (y_prompt, y_sample)
```
